# Optimizing a Trainium2 kernel written in Bass

```python
import jax, jax.numpy as jnp
from jax import lax
import numpy as np

D_MODEL = 1024
BATCH = 8
SEQ = 2048
DEPTH = 4

GRID_W = 64
CTX_LEN = 256
HEAD_DIM = 64
MIX_WIDTH = D_MODEL
A_HEADS = MIX_WIDTH // (2 * HEAD_DIM)
A_WIDTH = A_HEADS * HEAD_DIM
DECAY_LORA = 64
ICLR_LORA = 64
GATE_LORA = 128
B_HEADS = MIX_WIDTH // (2 * HEAD_DIM)
B_KV_HEADS = 2
WINDOW = 128
BLOCK = 128
C_HEADS = MIX_WIDTH // (2 * HEAD_DIM)
NA_KH = 8
NA_KW = 16
NA_QC = 16
NA_KC = 32
D_HEADS = MIX_WIDTH // (2 * HEAD_DIM)
D_KV_HEADS = 2
D_FF = 4 * D_MODEL
ROPE_BASE = 10000.0
NORM_EPS = 1e-6
GN_EPS = 64e-5
NEG_INF = -1e30
ATTN_SCALE = HEAD_DIM ** -0.5
N_EVEN = (DEPTH + 1) // 2
N_ODD = DEPTH // 2
A_IN = 3 * A_WIDTH + 2 * DECAY_LORA + 2 * ICLR_LORA + GATE_LORA
B_IN = (B_HEADS + 2 * B_KV_HEADS) * HEAD_DIM
C_IN = 3 * C_HEADS * HEAD_DIM
D_IN = (D_HEADS + 2 * D_KV_HEADS) * HEAD_DIM
EVEN_IN = A_IN + B_IN
ODD_IN = C_IN + D_IN
A_SPLITS = (A_WIDTH, 2 * A_WIDTH, 3 * A_WIDTH, 3 * A_WIDTH + 2 * DECAY_LORA,
            3 * A_WIDTH + 2 * DECAY_LORA + 2 * ICLR_LORA)

kernel_name = "hybrid_rwkv7_window_natten_gqa_diffusion_trunk"


def rms_norm(x, g):
    xf = x.astype(jnp.float32)
    y = xf * lax.rsqrt(jnp.mean(xf * xf, axis=-1, keepdims=True) + NORM_EPS)
    return (y * g.astype(jnp.float32)).astype(x.dtype)


def modulate(h, gain, shift, scale):
    return rms_norm(h, gain) * (1 + scale) + shift


def sqrelu_mlp(u, w1, w2):
    return jnp.square(jax.nn.relu(u @ w1)) @ w2


def axial_angles(n_tok):
    t = jnp.arange(n_tok, dtype=jnp.int32)
    n_freq = HEAD_DIM // 4
    inv_freq = ROPE_BASE ** (-jnp.arange(n_freq, dtype=jnp.float32) / n_freq)
    row = (t // GRID_W).astype(jnp.float32)[:, None] * inv_freq
    col = (t % GRID_W).astype(jnp.float32)[:, None] * inv_freq
    return row, col


def _rotate(x, ang):
    f = ang.shape[-1]
    cos = jnp.cos(ang)[None, :, None, :]
    sin = jnp.sin(ang)[None, :, None, :]
    x1 = x[..., :f].astype(jnp.float32)
    x2 = x[..., f:].astype(jnp.float32)
    return jnp.concatenate([x1 * cos - x2 * sin, x1 * sin + x2 * cos], axis=-1).astype(x.dtype)


def axial_rope(x, ang):
    half = HEAD_DIM // 2
    return jnp.concatenate([_rotate(x[..., :half], ang[0]), _rotate(x[..., half:], ang[1])], axis=-1)


def split_qkv(p, hq, hkv):
    bn, s, _ = p.shape
    q, k, v = jnp.split(p, [hq * HEAD_DIM, (hq + hkv) * HEAD_DIM], axis=-1)
    return (q.reshape(bn, s, hq, HEAD_DIM), k.reshape(bn, s, hkv, HEAD_DIM), v.reshape(bn, s, hkv, HEAD_DIM))


def softmax_with_sink(s, sink):
    sk = jnp.broadcast_to(sink.astype(jnp.float32)[:, :, None, None], s.shape[:-1] + (1,))
    return jax.nn.softmax(jnp.concatenate([s, sk], axis=-1), axis=-1)[..., :-1]


def context_attention(q, k, v, sink=None):
    bn, l, h, d = q.shape
    hkv = k.shape[2]
    qg = q.reshape(bn, l, hkv, h // hkv, d)
    s = jnp.einsum('blkgd,bckd->bkglc', qg, k).astype(jnp.float32) * ATTN_SCALE
    p = jax.nn.softmax(s, axis=-1) if sink is None else softmax_with_sink(s, sink)
    o = jnp.einsum('bkglc,bckd->blkgd', p.astype(v.dtype), v)
    return o.reshape(bn, l, h * d)


def centred_shift(p, mu_prev, mu_next):
    prev = jnp.pad(p, ((0, 0), (1, 0), (0, 0)))[:, :-1]
    nxt = jnp.pad(p, ((0, 0), (0, 1), (0, 0)))[:, 1:]
    return p + mu_prev * (prev - p) + mu_next * (nxt - p)


def rwkv7_prepare(pa, prm):
    p = centred_shift(pa, prm['mu_prev'], prm['mu_next']).astype(jnp.float32)
    bn, s, _ = p.shape
    r, k, v, wl, al, gl = jnp.split(p, A_SPLITS, axis=-1)
    wl = wl.reshape(bn, s, 2, DECAY_LORA)
    al = al.reshape(bn, s, 2, ICLR_LORA)
    w_raw = prm['w0'] + jnp.einsum('bsdl,dlc->bsdc', jnp.tanh(wl), prm['w2'])
    decay = jnp.exp(-jnp.exp(-jax.nn.softplus(-w_raw) - 0.5))
    iclr = jax.nn.sigmoid(prm['a0'] + jnp.einsum('bsdl,dlc->bsdc', al, prm['a2']))
    g = jax.nn.sigmoid(gl) @ prm['g2']
    kk = (k * prm['k_k']).reshape(bn, s, A_HEADS, HEAD_DIM)
    kk = kk / jnp.maximum(jnp.sqrt(jnp.sum(kk * kk, axis=-1, keepdims=True)), 1e-12)
    k_dir = k[:, :, None, :] * (1 + (iclr - 1) * prm['k_a'])
    heads = lambda t: t.reshape(t.shape[:-1] + (A_HEADS, HEAD_DIM))
    return dict(r=heads(r), v=heads(v), g=g, kk=kk, k=heads(k_dir), w=heads(decay), a=heads(iclr))


def rwkv7_scan(f, d, s0, reverse):
    def step(state, inp):
        r_t, w_t, k_t, v_t, kk_t, a_t = inp
        sa = jnp.einsum('bhvk,bhk->bhv', state, -kk_t)
        state = (state * w_t[:, :, None, :] + sa[..., None] * (kk_t * a_t)[:, :, None, :]
                 + v_t[..., None] * k_t[:, :, None, :])
        return state, jnp.einsum('bhvk,bhk->bhv', state, r_t)
    seq = (f['r'], f['w'][:, :, d], f['k'][:, :, d], f['v'], f['kk'], f['a'][:, :, d])
    xs = tuple(jnp.moveaxis(t, 1, 0) for t in seq)
    s_last, ys = lax.scan(step, s0, xs, reverse=reverse)
    return s_last, jnp.moveaxis(ys, 0, 1)


def rwkv7_output(f, y, prm, dtype):
    bn, s = y.shape[:2]
    mu = jnp.mean(y, axis=-1, keepdims=True)
    var = jnp.mean(jnp.square(y - mu), axis=-1, keepdims=True)
    yn = ((y - mu) * lax.rsqrt(var + GN_EPS)).reshape(bn, s, A_WIDTH) * prm['gn_w'] + prm['gn_b']
    bonus = jnp.sum(f['r'][:, :, None] * f['k'] * prm['r_k'], axis=(2, 4))
    yn = yn + (bonus[..., None] * f['v']).reshape(bn, s, A_WIDTH)
    return (yn * f['g']).astype(dtype)


def rwkv7_mixer(pa_ctx, pa_lat, prm, need_ctx):
    fc = rwkv7_prepare(pa_ctx, prm)
    fl = rwkv7_prepare(pa_lat, prm)
    zero = jnp.zeros((pa_lat.shape[0], A_HEADS, HEAD_DIM, HEAD_DIM), jnp.float32)
    s_cf, yc_f = rwkv7_scan(fc, 0, zero, False)
    _, yl_f = rwkv7_scan(fl, 0, s_cf, False)
    s_cb, yc_b = rwkv7_scan(fc, 1, zero, True)
    _, yl_b = rwkv7_scan(fl, 1, s_cb, True)
    out_lat = rwkv7_output(fl, yl_f + yl_b, prm, pa_lat.dtype)
    out_ctx = rwkv7_output(fc, yc_f + yc_b, prm, pa_ctx.dtype) if need_ctx else None
    return out_lat, out_ctx


def banded_window_attention(q, k, v, k_ctx, v_ctx, sink):
    bn, s, h, d = q.shape
    hkv = k.shape[2]
    nb = s // BLOCK
    qb = q.reshape(bn, nb, BLOCK, hkv, h // hkv, d)

    def band(t):
        tp = jnp.pad(t, ((0, 0), (BLOCK, BLOCK), (0, 0), (0, 0)))
        return jnp.concatenate([tp[:, j * BLOCK: j * BLOCK + s].reshape(bn, nb, BLOCK, hkv, d)
                                for j in range(3)], axis=2)
    kb, vb = band(k), band(v)
    qi = np.arange(BLOCK)[:, None]
    kj = np.arange(3 * BLOCK)[None, :]
    kpos = (np.arange(nb) * BLOCK)[:, None, None] - BLOCK + kj[None]
    ok = (np.abs(kj - BLOCK - qi) <= WINDOW)[None] & (kpos >= 0) & (kpos < s)
    s_lat = jnp.einsum('bnqkgd,bnjkd->bnkgqj', qb, kb).astype(jnp.float32) * ATTN_SCALE
    s_lat = jnp.where(ok[None, :, None, None], s_lat, NEG_INF)
    s_ctx = jnp.einsum('bnqkgd,bckd->bnkgqc', qb, k_ctx).astype(jnp.float32) * ATTN_SCALE
    p = softmax_with_sink(jnp.concatenate([s_lat, s_ctx], axis=-1), sink)
    p_lat, p_ctx = p[..., :3 * BLOCK].astype(v.dtype), p[..., 3 * BLOCK:].astype(v.dtype)
    o = (jnp.einsum('bnkgqj,bnjkd->bnqkgd', p_lat, vb)
         + jnp.einsum('bnkgqc,bckd->bnqkgd', p_ctx, v_ctx))
    return o.reshape(bn, s, h * d)


def neighbourhood_tables(rows):
    kh = min(NA_KH, rows)
    r = np.arange(rows)
    row_idx = np.clip(r - kh // 2, 0, rows - kh)[:, None] + np.arange(kh)[None, :]
    dr = row_idx - r[:, None] + NA_KH - 1
    ncb = GRID_W // NA_QC
    q_col = np.arange(ncb)[:, None] * NA_QC + np.arange(NA_QC)[None, :]
    key_col = (np.clip(np.arange(ncb) * NA_QC - NA_KW // 2, 0, GRID_W - NA_KC)[:, None]
               + np.arange(NA_KC)[None, :])
    win_start = np.clip(q_col - NA_KW // 2, 0, GRID_W - NA_KW)[:, :, None]
    kc = key_col[:, None, :]
    col_ok = (kc >= win_start) & (kc < win_start + NA_KW)
    dc = np.clip(kc - q_col[:, :, None] + NA_KW - 1, 0, 2 * NA_KW - 2)
    return row_idx, dr, key_col, col_ok, dc


def neighbourhood_attention(q, k, v, k_ctx, v_ctx, rpb):
    bn, s, h, d = q.shape
    rows = s // GRID_W
    ncb = GRID_W // NA_QC
    row_idx, dr, key_col, col_ok, dc = neighbourhood_tables(rows)
    kh = row_idx.shape[1]

    def gather(t):
        return t.reshape(bn, rows, GRID_W, h, d)[:, row_idx][:, :, :, key_col]
    kg, vg = gather(k), gather(v)
    qg = q.reshape(bn, rows, ncb, NA_QC, h, d)
    s_lat = jnp.einsum('brmqhd,brtmjhd->brmhqtj', qg, kg).astype(jnp.float32) * ATTN_SCALE
    bias = rpb[:, dr[:, None, None, :, None], dc[None, :, :, None, :]]
    s_lat = s_lat + bias.transpose(1, 2, 0, 3, 4, 5).astype(jnp.float32)
    s_lat = jnp.where(col_ok[:, None, :, None, :], s_lat, NEG_INF)
    s_lat = s_lat.reshape(s_lat.shape[:-2] + (kh * NA_KC,))
    s_ctx = jnp.einsum('brmqhd,bchd->brmhqc', qg, k_ctx).astype(jnp.float32) * ATTN_SCALE
    p = jax.nn.softmax(jnp.concatenate([s_lat, s_ctx], axis=-1), axis=-1)
    p_lat = p[..., :kh * NA_KC].reshape(p.shape[:-1] + (kh, NA_KC)).astype(v.dtype)
    p_ctx = p[..., kh * NA_KC:].astype(v.dtype)
    o = (jnp.einsum('brmhqtj,brtmjhd->brmqhd', p_lat, vg)
         + jnp.einsum('brmhqc,bchd->brmqhd', p_ctx, v_ctx))
    return o.reshape(bn, s, h * d)


def global_block_attention(q, k, v, k_ctx, v_ctx):
    bn, s, h, d = q.shape
    hkv = k.shape[2]
    nb = s // BLOCK
    k_all = jnp.concatenate([k_ctx, k], axis=1)
    v_all = jnp.concatenate([v_ctx, v], axis=1)
    qb = q.reshape(bn, nb, BLOCK, hkv, h // hkv, d).transpose(1, 0, 2, 3, 4, 5)

    def one_block(qblk):
        sc = jnp.einsum('bqkgd,bskd->bkgqs', qblk, k_all).astype(jnp.float32) * ATTN_SCALE
        p = jax.nn.softmax(sc, axis=-1).astype(v_all.dtype)
        return jnp.einsum('bkgqs,bskd->bqkgd', p, v_all)
    o = lax.map(one_block, qb)
    return o.transpose(1, 0, 2, 3, 4, 5).reshape(bn, s, h * d)


def even_mixer(u_ctx, u_lat, w_in, w_out, a_prm, sink, rope_ang, need_ctx):
    p_ctx = u_ctx @ w_in
    p_lat = u_lat @ w_in
    a_lat, a_ctx = rwkv7_mixer(p_ctx[..., :A_IN], p_lat[..., :A_IN], a_prm, need_ctx)
    qc, kc, vc = split_qkv(p_ctx[..., A_IN:], B_HEADS, B_KV_HEADS)
    ql, kl, vl = split_qkv(p_lat[..., A_IN:], B_HEADS, B_KV_HEADS)
    ql, kl = axial_rope(ql, rope_ang), axial_rope(kl, rope_ang)
    sink_g = sink.reshape(B_KV_HEADS, B_HEADS // B_KV_HEADS)
    b_lat = banded_window_attention(ql, kl, vl, kc, vc, sink_g)
    out_lat = jnp.concatenate([a_lat, b_lat], axis=-1) @ w_out
    if not need_ctx:
        return out_lat, None
    b_ctx = context_attention(qc, kc, vc, sink_g)
    return out_lat, jnp.concatenate([a_ctx, b_ctx], axis=-1) @ w_out


def odd_mixer(u_ctx, u_lat, w_in, w_out, rpb, q_gain, k_gain, rope_ang, need_ctx):
    p_ctx = u_ctx @ w_in
    p_lat = u_lat @ w_in
    cq_c, ck_c, cv_c = split_qkv(p_ctx[..., :C_IN], C_HEADS, C_HEADS)
    cq_l, ck_l, cv_l = split_qkv(p_lat[..., :C_IN], C_HEADS, C_HEADS)
    c_lat = neighbourhood_attention(cq_l, ck_l, cv_l, ck_c, cv_c, rpb)
    dq_c, dk_c, dv_c = split_qkv(p_ctx[..., C_IN:], D_HEADS, D_KV_HEADS)
    dq_l, dk_l, dv_l = split_qkv(p_lat[..., C_IN:], D_HEADS, D_KV_HEADS)
    dq_c, dk_c = rms_norm(dq_c, q_gain), rms_norm(dk_c, k_gain)
    dq_l = axial_rope(rms_norm(dq_l, q_gain), rope_ang)
    dk_l = axial_rope(rms_norm(dk_l, k_gain), rope_ang)
    d_lat = global_block_attention(dq_l, dk_l, dv_l, dk_c, dv_c)
    out_lat = jnp.concatenate([c_lat, d_lat], axis=-1) @ w_out
    if not need_ctx:
        return out_lat, None
    c_ctx_o = context_attention(cq_c, ck_c, cv_c)
    d_ctx_o = context_attention(dq_c, dk_c, dv_c)
    return out_lat, jnp.concatenate([c_ctx_o, d_ctx_o], axis=-1) @ w_out


def setup_inputs(seed: int = 0) -> dict:
    key = jax.random.key(seed)
    ks = iter(jax.random.split(key, 40))
    nrm = lambda shape, scale: scale * jax.random.normal(next(ks), shape, jnp.float32)
    unif = lambda shape, lo, hi: jax.random.uniform(next(ks), shape, jnp.float32, lo, hi)
    D = D_MODEL
    return {
        "x": nrm((BATCH, SEQ, D), 1.0),
        "c": nrm((BATCH, D), 1.0),
        "ctx": nrm((BATCH, CTX_LEN, D), 1.0),
        "c_ctx": nrm((D,), 1.0),
        "w_ada": nrm((DEPTH, D, 6 * D), 0.5 * D ** -0.5),
        "b_ada": nrm((DEPTH, 6 * D), 0.02),
        "g_pre_mix": 1.0 + nrm((DEPTH, D), 0.05),
        "g_post_mix": 1.0 + nrm((DEPTH, D), 0.05),
        "g_pre_ff": 1.0 + nrm((DEPTH, D), 0.05),
        "g_post_ff": 1.0 + nrm((DEPTH, D), 0.05),
        "w_in_even": nrm((N_EVEN, D, EVEN_IN), D ** -0.5),
        "w_in_odd": nrm((N_ODD, D, ODD_IN), D ** -0.5),
        "w_out": nrm((DEPTH, MIX_WIDTH, D), MIX_WIDTH ** -0.5),
        "w_ff1": nrm((DEPTH, D, D_FF), D ** -0.5),
        "w_ff2": nrm((DEPTH, D_FF, D), D_FF ** -0.5),
        "a_mu_prev": unif((N_EVEN, A_IN), 0.05, 0.45),
        "a_mu_next": unif((N_EVEN, A_IN), 0.05, 0.45),
        "a_w0": unif((N_EVEN, 2, A_WIDTH), -5.0, -0.5),
        "a_w2": nrm((N_EVEN, 2, DECAY_LORA, A_WIDTH), 0.1 * DECAY_LORA ** -0.5),
        "a_a0": nrm((N_EVEN, 2, A_WIDTH), 0.1),
        "a_a2": nrm((N_EVEN, 2, ICLR_LORA, A_WIDTH), 0.5 * ICLR_LORA ** -0.5),
        "a_g2": nrm((N_EVEN, GATE_LORA, A_WIDTH), GATE_LORA ** -0.5),
        "a_k_k": 0.85 + nrm((N_EVEN, A_WIDTH), 0.05),
        "a_k_a": 1.0 + nrm((N_EVEN, A_WIDTH), 0.05),
        "a_r_k": nrm((N_EVEN, A_HEADS, HEAD_DIM), 0.1),
        "a_gn_w": 1.0 + nrm((N_EVEN, A_WIDTH), 0.05),
        "a_gn_b": nrm((N_EVEN, A_WIDTH), 0.02),
        "b_sink": nrm((N_EVEN, B_HEADS), 0.5),
        "c_rpb": nrm((N_ODD, C_HEADS, 2 * NA_KH - 1, 2 * NA_KW - 1), 0.5),
        "d_q_gain": 1.0 + nrm((N_ODD, HEAD_DIM), 0.05),
        "d_k_gain": 1.0 + nrm((N_ODD, HEAD_DIM), 0.05),
    }


def reference(x, c, ctx, c_ctx, w_ada, b_ada, g_pre_mix, g_post_mix, g_pre_ff, g_post_ff,
              w_in_even, w_in_odd, w_out, w_ff1, w_ff2,
              a_mu_prev, a_mu_next, a_w0, a_w2, a_a0, a_a2, a_g2, a_k_k, a_k_a, a_r_k, a_gn_w, a_gn_b,
              b_sink, c_rpb, d_q_gain, d_k_gain):
    rope_ang = axial_angles(x.shape[1])
    s_lat = jax.nn.silu(c)[:, None, :]
    s_ctx = jax.nn.silu(c_ctx)[None, None, :]
    h_lat, h_ctx = x, ctx
    for i in range(DEPTH):
        need_ctx = i < DEPTH - 1
        j = i // 2
        ml = jnp.split(s_lat @ w_ada[i] + b_ada[i], 6, axis=-1)
        mc = jnp.split(s_ctx @ w_ada[i] + b_ada[i], 6, axis=-1)
        u_lat = modulate(h_lat, g_pre_mix[i], ml[0], ml[1])
        u_ctx = modulate(h_ctx, g_pre_mix[i], mc[0], mc[1])
        if i % 2 == 0:
            a_prm = dict(mu_prev=a_mu_prev[j], mu_next=a_mu_next[j], w0=a_w0[j], w2=a_w2[j],
                         a0=a_a0[j], a2=a_a2[j], g2=a_g2[j], k_k=a_k_k[j], k_a=a_k_a[j],
                         r_k=a_r_k[j], gn_w=a_gn_w[j], gn_b=a_gn_b[j])
            o_lat, o_ctx = even_mixer(u_ctx, u_lat, w_in_even[j], w_out[i], a_prm, b_sink[j], rope_ang, need_ctx)
        else:
            o_lat, o_ctx = odd_mixer(u_ctx, u_lat, w_in_odd[j], w_out[i], c_rpb[j], d_q_gain[j], d_k_gain[j],
                                     rope_ang, need_ctx)
        h_lat = h_lat + ml[2] * rms_norm(o_lat, g_post_mix[i])
        f_lat = sqrelu_mlp(modulate(h_lat, g_pre_ff[i], ml[3], ml[4]), w_ff1[i], w_ff2[i])
        h_lat = h_lat + ml[5] * rms_norm(f_lat, g_post_ff[i])
        if need_ctx:
            h_ctx = h_ctx + mc[2] * rms_norm(o_ctx, g_post_mix[i])
            f_ctx = sqrelu_mlp(modulate(h_ctx, g_pre_ff[i], mc[3], mc[4]), w_ff1[i], w_ff2[i])
            h_ctx = h_ctx + mc[5] * rms_norm(f_ctx, g_post_ff[i])
    return h_lat
```

```python
import contextlib
import numpy as np
import concourse.bass as bass
import concourse.mybir as mybir
from concourse.bass_utils import run_bass_kernel_spmd

F32 = mybir.dt.float32
BF16 = mybir.dt.bfloat16
AF = mybir.ActivationFunctionType
ALU = mybir.AluOpType
AX = mybir.AxisListType

D = 1024
NT = 18
TOK = NT * 128
DEPTH = 4
DFF = 4096
EPS = 1e-6


class KB:
    def __init__(s, nc, es):
        s.nc = nc
        s.es = es
        s.E = {'pe': nc.tensor, 'act': nc.scalar, 'dve': nc.vector, 'pool': nc.gpsimd, 'sp': nc.sync}
        s.sem = {}
        s.cnt = {e: 0 for e in s.E}
        s.known = {e: {} for e in s.E}
        s.lastw = {}
        s.readers = {}
        s.ndma = 24
        s.dma_uses = [0] * s.ndma
        s.dma_next = 0
        s.same_sync = {'pe': False, 'act': False, 'dve': True, 'pool': False, 'sp': False}
        s.nins = 0

    EP = 16384

    def _sem(s, key):
        if key not in s.sem:
            nm = "_".join(str(k) for k in key)
            s.sem[key] = s.es.enter_context(s.nc.semaphore("s_" + nm))
        return s.sem[key]

    def _need(s, e, F, c):
        if F[0] == 'e' and F[1] == e and not s.same_sync[e]:
            return False
        k = s.known[e]
        if F[0] == 'e':
            if k.get(('ep', F[1]), -1) > F[2]:
                return False
        if k.get(F, 0) >= c:
            return False
        k[F] = c
        if F[0] == 'e':
            k[('ep', F[1])] = max(k.get(('ep', F[1]), -1), F[2])
        return True

    def _wait(s, e, F, c):
        if s._need(e, F, c):
            s.E[e].wait_ge(s._sem(F), c)
            s.nins += 1

    def _deps(s, e, r, w, extra=()):
        need = {}
        for (F, c) in extra:
            need[F] = max(need.get(F, 0), c)
        for x in r:
            t = s.lastw.get(x)
            if t is not None:
                need[t[0]] = max(need.get(t[0], 0), t[1])
        for x in w:
            t = s.lastw.get(x)
            if t is not None:
                need[t[0]] = max(need.get(t[0], 0), t[1])
            for F, c in s.readers.get(x, {}).items():
                need[F] = max(need.get(F, 0), c)
        lst = [(F, c) for F, c in need.items() if s._need(e, F, c)]
        for F, c in lst[:-1]:
            s.E[e].wait_ge(s._sem(F), c)
            s.nins += 1
        return lst[-1] if lst else None

    def _upd(s, tok, r, w):
        for x in r:
            d = s.readers.setdefault(x, {})
            d[tok[0]] = max(d.get(tok[0], 0), tok[1])
        for x in w:
            s.lastw[x] = tok
            s.readers[x] = {}

    def op(s, e, fn, r=(), w=(), inc=True):
        lw = s._deps(e, r, w)
        ins = fn(s.E[e])
        if lw is not None:
            ins._wait_ge(s._sem(lw[0]), lw[1])
        F = ('e', e, s.cnt[e] // s.EP)
        c = s.cnt[e] % s.EP + 1
        s.nins += 1
        if inc:
            s.cnt[e] += 1
            ins.then_inc(s._sem(F), 1)
        tok = (F, c)
        s._upd(tok, r, w)
        return tok

    def dma(s, q, out, in_, r=(), w=(), **kw):
        i = s.dma_next
        s.dma_next = (i + 1) % s.ndma
        F = ('d', i)
        extra = [(F, 16 * s.dma_uses[i])] if s.dma_uses[i] > 0 else []
        lw = s._deps(q, r, w, extra)
        ins = s.E[q].dma_start(out=out, in_=in_, **kw)
        if lw is not None:
            ins._wait_ge(s._sem(lw[0]), lw[1])
        s.nins += 1
        s.dma_uses[i] += 1
        ins.then_inc(s._sem(F), 16)
        tok = (F, 16 * s.dma_uses[i])
        s._upd(tok, r, w)
        return tok

    def wait_all(s, e, keys):
        need = {}
        for x in keys:
            t = s.lastw.get(x)
            if t is not None:
                need[t[0]] = max(need.get(t[0], 0), t[1])
            for F, c in s.readers.get(x, {}).items():
                need[F] = max(need.get(F, 0), c)
        for F, c in need.items():
            s._wait(e, F, c)

    def barrier(s, keys):
        for e in ('pe', 'dve', 'act', 'pool', 'sp'):
            s.wait_all(e, keys)


def build(nlayers=DEPTH, mix=True, dbg=None, rwkv=True):
    nc = bass.Bass("TRN2", target_bir_lowering=False)
    dt_in = lambda name, shape: nc.dram_tensor(name, list(shape), F32, kind="ExternalInput").ap()
    x_d = dt_in("x", (2048, D))
    ctx_d = dt_in("ctx", (256, D))
    cvec_d = dt_in("cvec", (128, 8, 2))
    wada_d = dt_in("w_ada", (DEPTH, D, 6 * D))
    bada_d = dt_in("b_ada", (DEPTH, 6 * D))
    gains_d = dt_in("gains2", (DEPTH, 2, 4, D))
    wff1_d = dt_in("w_ff1", (DEPTH, D, DFF))
    wff2_d = dt_in("w_ff2", (DEPTH, DFF, D))
    identf_d = dt_in("identf_in", (128, 128))
    sel_d = dt_in("sel_in", (2, 2, 128))
    wout_d = dt_in("w_out", (DEPTH, D, D))
    wev_d = dt_in("w_even_x", (2, D, 26 * 128))
    wmask_d = dt_in("wmask_in", (128, 6, 512))
    sink_d = dt_in("b_sink", (2, 8))
    ropeC_d = dt_in("ropeC_in", (128, TOK))
    ropeS_d = dt_in("ropeS_in", (128, TOK))
    wod_d = dt_in("w_odd_x", (2, D, 23 * 128))
    cbias_d = dt_in("cbias_in", (2, 8, 128, (NCT + 1) * 128))
    dgain_d = dt_in("dgain_in", (2, 128, 4))
    rwp_d = dt_in("rwp_in", (2, 128, 66))
    w2_d = dt_in("a_w2s", (2, 128, 512))
    a2_d = dt_in("a_a2s", (2, 128, 512))
    g2_d = dt_in("a_g2", (2, 128, 512))
    mall_d = dt_in("mall_in", (128, 2, 512))
    mscan_d = dt_in("mscan_in", (128, 512))
    lmk_d = dt_in("lmk_in", (128, 2, 7, 128))
    blkf_d = dt_in("blkf_in", (128, 128))
    pA_d = nc.dram_tensor("pA_scratch", [15, 128, TOK], BF16, kind="Internal").ap()
    w1b_d = nc.dram_tensor("w1b_scratch", [DEPTH, D, DFF], BF16, kind="Internal").ap()
    w2b_d = nc.dram_tensor("w2b_scratch", [DEPTH, DFF, D], BF16, kind="Internal").ap()
    out_d = nc.dram_tensor("out", [2048, D], F32, kind="ExternalOutput").ap()
    if dbg in ('mixT', 'mixT1', 'mixT2', 'mixT3'):
        dbg_d = nc.dram_tensor("dbg", [128, 8, TOK], BF16, kind="ExternalOutput").ap()

    _uid = [0]

    def SBT(name, shape, dt):
        _uid[0] += 1
        return nc.sbuf_tensor(f"{name}_{_uid[0]}", shape, dt)

    with contextlib.ExitStack() as es:
        kb = KB(nc, es)
        sb = lambda name, shape, dt=F32: es.enter_context(SBT(name, list(shape), dt))
        h = sb("h", (128, NT, D))
        identf = sb("identf", (128, 128))
        identb = sb("identb", (128, 128), BF16)
        sel = sb("sel", (2, 2, 128))
        ones_f = sb("ones_f", (128, 128))
        ones_b = sb("ones_b", (128, 128), BF16)
        sT = sb("sT", (128, 8, 2))
        modT = sb("modT", (128, 4, 8, 2))
        G = sb("G", (128, 4, D))
        ss = sb("ss", (128, 4))
        epsT = sb("epsT", (128, 2))
        rstd = sb("rstd", (128, 2))
        U = {'t': None}

        @contextlib.contextmanager
        def uT_scope():
            with SBT("uT", [128, 8, TOK], BF16) as t:
                U['t'] = t
                yield t
                kb.barrier([('uT', i) for i in range(NT)])
            U['t'] = None
        RP = {}
        ps = [es.enter_context(nc.psum_tensor(f"ps{i}", [128, 512], F32)) for i in range(8)]
        psn = [0]

        def nps():
            i = psn[0]
            psn[0] = (i + 1) % 8
            return i

        xv = x_d.rearrange("(t p) d -> p t d", p=128)
        cv = ctx_d.rearrange("(t p) d -> p t d", p=128)
        for t in range(2):
            kb.dma('sp', h[:, t, :], cv[:, t, :], w=[('h', t)])
        for t in range(16):
            kb.dma('sp', h[:, 2 + t, :], xv[:, t, :], w=[('h', 2 + t)])
        kb.dma('sp', identf[:], identf_d, w=['identf'])
        kb.dma('sp', sel[:], sel_d, w=['sel'])
        kb.dma('sp', sT[:], cvec_d, w=['sT'])
        kb.op('dve', lambda e: e.tensor_copy(out=identb[:], in_=identf[:]), r=['identf'], w=['identb'])
        kb.op('dve', lambda e: e.memset(ones_f[:], 1.0), w=['ones_f'])
        kb.op('dve', lambda e: e.memset(epsT[:], EPS), w=['epsT'])
        kb.op('dve', lambda e: e.memset(ones_b[:], 1.0), w=['ones_b'])
        kb.op('act', lambda e: e.activation(out=sT[:], in_=sT[:], func=AF.Silu), r=['sT'], w=['sT'])

        def adaln(L):
            with contextlib.ExitStack() as es2:
                sb2 = lambda name, shape, dt=F32: es2.enter_context(SBT(name, list(shape), dt))
                modrow = sb2("modrow", (2, 6 * D))
                grow = sb2("grow", (2, 4, D))
                wt = [sb2(f"wada{i}", (128, 9, 512)) for i in range(2)]
                kb.dma('sp', grow[:], gains_d[L], w=['grow'])
                wv = wada_d[L].rearrange("(kc p) n -> p kc n", p=128)
                for g in range(12):
                    b = g % 2
                    kb.dma('sp', wt[b][:, 0:8, :], wv[:, :, g * 512:(g + 1) * 512], w=[('wada', b)])
                    kb.dma('sp', wt[b][0:1, 8, :], bada_d[L:L + 1, g * 512:(g + 1) * 512], w=[('wadab', b)])
                    pi = nps()
                    for kc in range(8):
                        kb.op('pe', lambda e, kc=kc, b=b, pi=pi: e.matmul(
                            ps[pi][0:2, :], lhsT=sT[:, kc, :], rhs=wt[b][:, kc, :], start=(kc == 0), stop=False),
                            r=['sT', ('wada', b)], w=[('ps', pi)], inc=False)
                    kb.op('pe', lambda e, b=b, pi=pi: e.matmul(
                        ps[pi][0:2, :], lhsT=ones_f[0:1, 0:2], rhs=wt[b][0:1, 8, :], start=False, stop=True),
                        r=['ones_f', ('wadab', b)], w=[('ps', pi)])
                    kb.op('dve', lambda e, g=g, pi=pi: e.tensor_copy(
                        out=modrow[:, g * 512:(g + 1) * 512], in_=ps[pi][0:2, :]),
                        r=[('ps', pi)], w=[('modrow', g)])
                seg = lambda i: modrow[:, i * D:(i + 1) * D]
                mk = lambda i: [('modrow', 2 * i), ('modrow', 2 * i + 1)]
                kb.op('dve', lambda e: e.scalar_tensor_tensor(out=seg(1), in0=seg(1), scalar=1.0, in1=grow[:, 0, :],
                                                              op0=ALU.add, op1=ALU.mult), r=mk(1) + ['grow'], w=mk(1))
                kb.op('dve', lambda e: e.tensor_tensor(out=seg(2), in0=seg(2), in1=grow[:, 1, :], op=ALU.mult),
                      r=mk(2) + ['grow'], w=mk(2))
                kb.op('dve', lambda e: e.scalar_tensor_tensor(out=seg(4), in0=seg(4), scalar=1.0, in1=grow[:, 2, :],
                                                              op0=ALU.add, op1=ALU.mult), r=mk(4) + ['grow'], w=mk(4))
                kb.op('dve', lambda e: e.tensor_tensor(out=seg(5), in0=seg(5), in1=grow[:, 3, :], op=ALU.mult),
                      r=mk(5) + ['grow'], w=mk(5))
                pi = nps()
                for vi, sg in enumerate((1, 0, 4, 3)):
                    for kc in range(8):
                        kb.op('pe', lambda e, vi=vi, sg=sg, kc=kc, pi=pi: e.transpose(
                            ps[pi][:, (vi * 8 + kc) * 2:(vi * 8 + kc) * 2 + 2],
                            modrow[:, sg * D + kc * 128: sg * D + (kc + 1) * 128], identf[0:2, 0:2]),
                            r=mk(sg) + ['identf'], w=[('ps', pi)])
                kb.op('dve', lambda e, pi=pi: e.tensor_copy(out=modT[:].rearrange("p a b c -> p (a b c)"),
                                                           in_=ps[pi][:, 0:64]), r=[('ps', pi)], w=['modT'])
                for gi, sg in enumerate((2, 5)):
                    for st in range(2):
                        for hh in range(2):
                            pi = nps()
                            kb.op('pe', lambda e, sg=sg, st=st, hh=hh, pi=pi: e.matmul(
                                ps[pi][:, :], lhsT=sel[:, st, :], rhs=modrow[:, sg * D + hh * 512: sg * D + (hh + 1) * 512],
                                start=True, stop=True), r=mk(sg) + ['sel'], w=[('ps', pi)])
                            kb.op('act', lambda e, gi=gi, st=st, hh=hh, pi=pi: e.copy(
                                out=G[:, gi * 2 + st, hh * 512:(hh + 1) * 512], in_=ps[pi][:, :]),
                                r=[('ps', pi)], w=[('G', gi * 2 + st)])
                kb.barrier(['grow', ('wada', 0), ('wada', 1), ('wadab', 0), ('wadab', 1)] + [('modrow', i) for i in range(12)])

        def prep(vi):
            with contextlib.ExitStack() as esx:
                xn = [esx.enter_context(SBT(f"xn{i}", [128, D], BF16)) for i in range(2)]
                prep_(vi, xn)
                kb.barrier([('xn', 0), ('xn', 1)])

        def prep_(vi, xn):
            def stage_a(t):
                b = t % 2
                kb.op('act', lambda e: e.activation(out=xn[b][:], in_=h[:, t, :], func=AF.Square,
                                                    accum_out=ss[:, b:b + 1]),
                      r=[('h', t)], w=[('xn', b), ('ss', b)])
                kb.op('act', lambda e: e.activation(out=rstd[:, b:b + 1], in_=ss[:, b:b + 1], func=AF.Sqrt,
                                                    bias=epsT[:, 0:1], scale=1.0 / D),
                      r=[('ss', b), 'epsT'], w=[('rstd', b)])
                kb.op('dve', lambda e: e.reciprocal(out=rstd[:, b:b + 1], in_=rstd[:, b:b + 1]),
                      r=[('rstd', b)], w=[('rstd', b)])
                kb.op('act', lambda e: e.activation(out=xn[b][:], in_=h[:, t, :], func=AF.Copy,
                                                    scale=rstd[:, b:b + 1]),
                      r=[('h', t), ('rstd', b)], w=[('xn', b)])

            def stage_b(t):
                st = 1 if t < 2 else 0
                b = t % 2
                pi = nps()
                pb = ps[pi][:].bitcast(BF16)
                for kc in range(8):
                    kb.op('pe', lambda e, kc=kc: e.transpose(pb[:, kc * 128:(kc + 1) * 128],
                                                             xn[b][:, kc * 128:(kc + 1) * 128], identb[:]),
                          r=[('xn', b), 'identb'], w=[('ps', pi)], inc=(kc == 7))
                for kc in range(8):
                    dstk = U['t'][:, kc, t * 128:(t + 1) * 128]
                    Ak = modT[:, vi, kc, st:st + 1]
                    Sk = modT[:, vi + 1, kc, st:st + 1]
                    if kc % 2 == 0:
                        kb.op('act', lambda e, kc=kc, dstk=dstk, Ak=Ak, Sk=Sk: e.activation(
                            out=dstk, in_=pb[:, kc * 128:(kc + 1) * 128], func=AF.Identity, bias=Sk, scale=Ak),
                            r=[('ps', pi), 'modT'], w=[('uT', t)])
                    else:
                        kb.op('dve', lambda e, kc=kc, dstk=dstk, Ak=Ak, Sk=Sk: e.tensor_scalar(
                            out=dstk, in0=pb[:, kc * 128:(kc + 1) * 128], scalar1=Ak, scalar2=Sk, op0=ALU.mult, op1=ALU.add),
                            r=[('ps', pi), 'modT'], w=[('uT', t)])
            stage_a(0)
            for t in range(NT):
                if t + 1 < NT:
                    stage_a(t + 1)
                stage_b(t)

        def resid(t, pis, gi, ftmp):
            st = 1 if t < 2 else 0
            for hh in range(2):
                kb.op('act', lambda e, hh=hh: e.activation(out=ftmp[:, hh * 512:(hh + 1) * 512], in_=ps[pis[hh]][:, :],
                                                          func=AF.Square, accum_out=ss[:, 2 + hh:3 + hh]),
                      r=[('ps', pis[hh])], w=[('ss', 2 + hh), 'ftmp'])
            kb.op('dve', lambda e: e.tensor_tensor(out=ss[:, 2:3], in0=ss[:, 2:3], in1=ss[:, 3:4], op=ALU.add),
                  r=[('ss', 2), ('ss', 3)], w=[('ss', 2)])
            kb.op('act', lambda e: e.activation(out=ss[:, 2:3], in_=ss[:, 2:3], func=AF.Sqrt, bias=epsT[:, 0:1],
                                                scale=1.0 / D), r=[('ss', 2), 'epsT'], w=[('ss', 2)])
            kb.op('dve', lambda e: e.reciprocal(out=ss[:, 2:3], in_=ss[:, 2:3]), r=[('ss', 2)], w=[('ss', 2)])
            for hh in range(2):
                sl = slice(hh * 512, (hh + 1) * 512)
                kb.op('dve', lambda e, hh=hh, sl=sl: e.scalar_tensor_tensor(
                    out=ftmp[:, sl], in0=ps[pis[hh]][:, :], scalar=ss[:, 2:3], in1=G[:, gi * 2 + st, sl],
                    op0=ALU.mult, op1=ALU.mult), r=[('ps', pis[hh]), ('ss', 2), ('G', gi * 2 + st)], w=['ftmp'])
            kb.op('pool', lambda e: e.tensor_tensor(out=h[:, t, :], in0=h[:, t, :], in1=ftmp[:], op=ALU.add),
                  r=['ftmp', ('h', t)], w=[('h', t)])

        def ffn_convert(L):
            for i in range(8):
                kb.dma('pool', w1b_d[L, i * 128:(i + 1) * 128, :], wff1_d[L, i * 128:(i + 1) * 128, :], w=[('w1b', L)])
            for i in range(8):
                kb.dma('pool', w2b_d[L, i * 512:(i + 1) * 512, :], wff2_d[L, i * 512:(i + 1) * 512, :], w=[('w2b', L)])

        def ffn(L):
            with contextlib.ExitStack() as es2:
                sb2 = lambda name, shape, dt=F32: es2.enter_context(SBT(name, list(shape), dt))
                h1T = sb2("h1T", (128, 32, 512), BF16)
                w1 = [sb2(f"w1_{i}", (128, 8, 512), BF16) for i in range(2)]
                w2 = [sb2(f"w2_{i}", (128, 4, D), BF16) for i in range(2)]
                rl = [sb2(f"rl{i}", (128, 512)) for i in range(2)]
                ftmp = sb2("ftmp", (128, D))
                w1v = w1b_d[L].rearrange("(kc p) n -> p kc n", p=128)
                w2v = w2b_d[L].rearrange("(fc p) n -> p fc n", p=128)
                groups = [(0, 512), (512, 512), (1024, 512), (1536, 512), (2048, 256)]
                if L == DEPTH - 1:
                    groups = [(256, 512), (768, 512), (1280, 512), (1792, 512)]
                nld = [0, 0]
                for (t0, tn) in groups:
                    for fg in range(8):
                        b = nld[0] % 2
                        nld[0] += 1
                        kb.dma('sp', w1[b][:], w1v[:, :, fg * 512:(fg + 1) * 512], r=[('w1b', L)], w=[('w1', b)])
                        for f4 in range(4):
                            fc = fg * 4 + f4
                            pi = nps()
                            for kc in range(8):
                                kb.op('pe', lambda e, kc=kc, b=b, f4=f4, pi=pi: e.matmul(
                                    ps[pi][:, 0:tn], lhsT=w1[b][:, kc, f4 * 128:(f4 + 1) * 128], rhs=U['t'][:, kc, t0:t0 + tn],
                                    start=(kc == 0), stop=(kc == 7)),
                                    r=[('w1', b)] + [('uT', t) for t in range(t0 // 128, (t0 + tn) // 128)], w=[('ps', pi)],
                                    inc=(kc == 7))
                            rb = fc % 2
                            kb.op('act', lambda e, rb=rb, pi=pi: e.activation(out=rl[rb][:, 0:tn], in_=ps[pi][:, 0:tn],
                                                                             func=AF.Relu), r=[('ps', pi)], w=[('rl', rb)])
                            kb.op('pool', lambda e, rb=rb, fc=fc: e.tensor_tensor(
                                out=h1T[:, fc, 0:tn], in0=rl[rb][:, 0:tn], in1=rl[rb][:, 0:tn], op=ALU.mult),
                                r=[('rl', rb)], w=[('h1T', fc)])
                    ntile = tn // 128
                    for fg in range(8):
                        b = nld[1] % 2
                        nld[1] += 1
                        kb.dma('sp', w2[b][:], w2v[:, fg * 4:(fg + 1) * 4, :], r=[('w2b', L)], w=[('w2', b)])
                        for f4 in range(4):
                            fc = fg * 4 + f4
                            for tt in range(ntile):
                                for hh in range(2):
                                    pi = tt * 2 + hh
                                    kb.op('pe', lambda e, fc=fc, f4=f4, tt=tt, hh=hh, pi=pi, b=b: e.matmul(
                                        ps[pi][:, :], lhsT=h1T[:, fc, tt * 128:(tt + 1) * 128],
                                        rhs=w2[b][:, f4, hh * 512:(hh + 1) * 512], start=(fc == 0), stop=(fc == 31)),
                                        r=[('h1T', fc), ('w2', b)], w=[('ps', pi)],
                                        inc=(fc == 31 or (tt == ntile - 1 and hh == 1)))
                    for tt in range(ntile):
                        resid(t0 // 128 + tt, (tt * 2, tt * 2 + 1), 1, ftmp)
                kb.barrier([('w1', 0), ('w1', 1), ('w2', 0), ('w2', 1), ('rl', 0), ('rl', 1), 'ftmp']
                           + [('h1T', i) for i in range(32)])


        GROUPS = [(0, 256), (256, 512), (768, 512), (1280, 512), (1792, 512)]

        def tiles_of(t0, tn):
            return list(range(t0 // 128, (t0 + tn) // 128))

        def proj_fm(wsrc, nchunks, wb, evac, groups=GROUPS):
            wv = wsrc.rearrange("(kc p) n -> p kc n", p=128)
            for cg in range(0, nchunks, 1):
                n = 1
                b = wb['n'] % len(wb['bufs'])
                wb['n'] += 1
                buf = wb['bufs'][b]
                kb.dma('pool', buf[:, :, 0:n * 128], wv[:, :, cg * 128:(cg + n) * 128], w=[('wb', b)])
                for c in range(n):
                    for (t0, tn) in groups:
                        pi = nps()
                        for kc in range(8):
                            kb.op('pe', lambda e, kc=kc, c=c, pi=pi, t0=t0, tn=tn, buf=buf: e.matmul(
                                ps[pi][:, 0:tn], lhsT=buf[:, kc, c * 128:(c + 1) * 128], rhs=U['t'][:, kc, t0:t0 + tn],
                                start=(kc == 0), stop=(kc == 7)),
                                r=[('wb', b)] + [('uT', t) for t in tiles_of(t0, tn)], w=[('ps', pi)], inc=(kc == 7))
                        evac(cg + c, t0, tn, pi)

        def proj_tm(wsrc, ncols, wb, evac):
            wv = wsrc.rearrange("(kc p) n -> p kc n", p=128)
            b = wb['n'] % len(wb['bufs'])
            wb['n'] += 1
            buf = wb['bufs'][b]
            kb.dma('pool', buf[:, :, 0:ncols], wv, w=[('wb', b)])
            for t in range(NT):
                pi = nps()
                for kc in range(8):
                    kb.op('pe', lambda e, kc=kc, pi=pi, t=t: e.matmul(
                        ps[pi][:, 0:ncols], lhsT=U['t'][:, kc, t * 128:(t + 1) * 128], rhs=buf[:, kc, 0:ncols],
                        start=(kc == 0), stop=(kc == 7)), r=[('wb', b), ('uT', t)], w=[('ps', pi)], inc=(kc == 7))
                evac(t, pi)

        def attn_group(A, qT, qkey, base, kT, kkey, vt2, vkey, q0, nq, keys, dst, dkey, seed=None):
            rows = slice(base, base + 64)
            g = A['g'] % 4
            A['g'] += 1
            pnum = g
            drows = slice(64 - base, 128 - base)
            first = True
            nk = len(keys)
            LA = 3
            sbank = {}

            def crange(ki):
                k = keys[ki]
                return k[2] if len(k) > 2 else (0, nq)

            def issue_S(ki):
                kt = keys[ki][0]
                c0, c1 = crange(ki)
                pi = 4 + (A['s'] % 4)
                A['s'] += 1
                sbank[ki] = pi
                kb.op('pe', lambda e: e.matmul(
                    ps[pi][:, c0:c1], lhsT=kT[rows, kt * 128:(kt + 1) * 128], rhs=qT[rows, q0 + c0:q0 + c1],
                    start=True, stop=True), r=[kkey, qkey], w=[('ps', pi)])
            for ki in range(min(LA, nk)):
                issue_S(ki)
            for ki in range(nk):
                kt, mask = keys[ki][0], keys[ki][1]
                c0, c1 = crange(ki)
                last = ki == nk - 1
                if ki + LA < nk:
                    issue_S(ki + LA)
                pi = sbank[ki]
                pb = A['p'] % len(A['PT'])
                A['p'] += 1
                PT = A['PT'][pb]
                kb.op('act', lambda e, pi=pi, PT=PT: e.activation(out=PT[:, c0:c1], in_=ps[pi][:, c0:c1], func=AF.Exp,
                                                                 scale=0.125), r=[('ps', pi)], w=[('PT', pb)])
                if mask is not None:
                    mk_ap, mk_key = mask
                    pieces = mk_ap if isinstance(mk_ap, list) else [(c0, c1, mk_ap)]
                    for (p0, p1, pap) in pieces:
                        kb.op('dve', lambda e, PT=PT, pap=pap, p0=p0, p1=p1: e.tensor_tensor(out=PT[:, p0:p1], in0=PT[:, p0:p1],
                                                                                          in1=pap, op=ALU.mult),
                              r=[('PT', pb), mk_key], w=[('PT', pb)])
                kb.op('pe', lambda e, kt=kt, PT=PT, first=first, last=last: e.matmul(
                    ps[pnum][:, c0:c1], lhsT=vt2[:, kt, base:base + 128], rhs=PT[:, c0:c1], start=first, stop=last),
                    r=[vkey, ('PT', pb)], w=[('ps', pnum)])
                first = False
            rd = A['rden']
            if seed is not None:
                kb.op('dve', lambda e: e.tensor_scalar(out=rd[drows, 0:nq], in0=ps[pnum][drows, 0:nq], scalar1=seed[drows, :],
                                                       scalar2=None, op0=ALU.add), r=[('ps', pnum), 'eskb'], w=['rden'])
                kb.op('dve', lambda e: e.reciprocal(out=rd[drows, 0:nq], in_=rd[drows, 0:nq]), r=['rden'], w=['rden'])
            else:
                kb.op('dve', lambda e: e.reciprocal(out=rd[drows, 0:nq], in_=ps[pnum][drows, 0:nq]),
                      r=[('ps', pnum)], w=['rden'])
            kb.op('dve', lambda e: e.tensor_tensor(out=dst, in0=ps[pnum][rows, 0:nq], in1=rd[drows, 0:nq], op=ALU.mult),
                  r=[('ps', pnum), 'rden'], w=[dkey])

        def rope(x, xp, xkey, xpkey):
            kb.op('dve', lambda e: e.tensor_tensor(out=x[:], in0=x[:], in1=RP['C'][:], op=ALU.mult), r=[xkey, 'rope'], w=[xkey])
            kb.op('pool', lambda e: e.tensor_tensor(out=xp[:], in0=xp[:], in1=RP['S'][:], op=ALU.mult), r=[xpkey, 'rope'], w=[xpkey])
            kb.op('dve', lambda e: e.tensor_tensor(out=x[:], in0=x[:], in1=xp[:], op=ALU.add), r=[xkey, xpkey], w=[xkey])

        def outproj(L, mixT):
            with contextlib.ExitStack() as es2:
                sb2 = lambda name, shape, dt=F32: es2.enter_context(SBT(name, list(shape), dt))
                wo = sb2("wo", (128, 8, D), BF16)
                ftmp = sb2("ftmp_o", (128, D))
                kb.dma('pool', wo[:, 0:4, :], wout_d[L, 0:512, :].rearrange("(kc p) n -> p kc n", p=128), w=['wo'])
                wo2 = wout_d[L, 512:1024, :].rearrange("(hh c p) n -> hh p c n", hh=2, p=64)
                for hh in range(2):
                    kb.dma('pool', wo[hh * 64:(hh + 1) * 64, 4:8, :], wo2[hh], w=['wo'])
                for t in range(NT):
                    pis = (nps(), nps())
                    for hh in range(2):
                        for c in range(8):
                            kb.op('pe', lambda e, c=c, hh=hh, t=t: e.matmul(
                                ps[pis[hh]][:, :], lhsT=mixT[:, c, t * 128:(t + 1) * 128], rhs=wo[:, c, hh * 512:(hh + 1) * 512],
                                start=(c == 0), stop=(c == 7)), r=['wo', ('mixT', c)], w=[('ps', pis[hh])], inc=(c == 7))
                    resid(t, pis, 0, ftmp)
                kb.barrier(['wo', 'ftmp'])


        def rwkv_project(L, es2):
            j = L // 2
            sb2 = lambda name, shape, dt=F32: es2.enter_context(SBT(name, list(shape), dt))
            wb = {'n': 0, 'bufs': [sb2(f"wbr{i}", (128, 8, 128), BF16) for i in range(2)]}
            praw = sb2("praw", (128, TOK))
            psh = sb2("psh", (128, TOK))
            pbf = [sb2(f"pbf{i}", (128, TOK), BF16) for i in range(2)]
            rwp = sb2("rwp_p", (128, 66))
            c0 = sb2("c0", (128, 15))
            kb.dma('sp', rwp[:], rwp_d[j], w=['rwp'])
            mu2 = rwp[:, 0:30].rearrange("p (c two) -> p c two", two=2)
            kb.op('dve', lambda e: e.tensor_tensor(out=c0[:], in0=mu2[:, :, 0], in1=mu2[:, :, 1], op=ALU.add), r=['rwp'], w=['c0'])
            kb.op('dve', lambda e: e.tensor_scalar(out=c0[:], in0=c0[:], scalar1=-1.0, scalar2=1.0, op0=ALU.mult, op1=ALU.add),
                  r=['c0'], w=['c0'])
            for c in range(15):
                def ev(cc, t0, tn, pi):
                    kb.op('act', lambda e: e.copy(out=praw[:, t0:t0 + tn], in_=ps[pi][:, 0:tn]), r=[('ps', pi)], w=['praw'])
                proj_fm(wev_d[j][:, c * 128:(c + 1) * 128], 1, wb, ev)
                mp = rwp[:, 2 * c:2 * c + 1]
                mn = rwp[:, 2 * c + 1:2 * c + 2]
                kb.op('dve', lambda e: e.tensor_scalar(out=psh[:], in0=praw[:], scalar1=c0[:, c:c + 1], scalar2=None, op0=ALU.mult),
                      r=['praw', 'c0'], w=['psh'])
                for (lo, hi) in ((0, 256), (256, TOK)):
                    kb.op('dve', lambda e: e.scalar_tensor_tensor(out=psh[:, lo + 1:hi], in0=praw[:, lo:hi - 1], scalar=mp,
                                                                  in1=psh[:, lo + 1:hi], op0=ALU.mult, op1=ALU.add),
                          r=['praw', 'rwp', 'psh'], w=['psh'])
                    kb.op('dve', lambda e: e.scalar_tensor_tensor(out=psh[:, lo:hi - 1], in0=praw[:, lo + 1:hi], scalar=mn,
                                                                  in1=psh[:, lo:hi - 1], op0=ALU.mult, op1=ALU.add),
                          r=['praw', 'rwp', 'psh'], w=['psh'])
                b = c % 2
                fn = {12: AF.Tanh, 14: AF.Sigmoid}.get(c, AF.Copy)
                kb.op('act', lambda e: e.activation(out=pbf[b][:], in_=psh[:], func=fn), r=['psh'], w=[('pbf', b)])
                kb.dma('sp', pA_d[c], pbf[b][:], r=[('pbf', b)], w=[('pA', c)])
            kb.barrier([('wb', 0), ('wb', 1), 'praw', 'psh', ('pbf', 0), ('pbf', 1), 'rwp', 'c0'])

        def rwkv_passes(L, mixT, es2):
            j = L // 2
            sb2 = lambda name, shape, dt=F32: es2.enter_context(SBT(name, list(shape), dt))
            keys_all = []

            def al(name, shape, dt=F32):
                keys_all.append(name)
                return sb2(name, shape, dt)
            rwp = al("rwp", (128, 66))
            w2s = al("w2s", (128, 128), BF16)
            a2s = al("a2s", (128, 128), BF16)
            g2s = al("g2s", (128, 128), BF16)
            mall = al("mall", (128, 512), BF16)
            mscan = al("mscan", (128, 256))
            blkf = al("blkf", (128, 128))
            blkRK = al("blkRK", (128, 128), BF16)
            gne = al("gne", (128, 1))
            kb.dma('sp', rwp[:], rwp_d[j], w=['rwp'])
            kb.dma('sp', mscan[:], mscan_d[:, 0:256], w=['mscan'])
            kb.dma('sp', blkf[:], blkf_d, w=['blkf'])
            kb.op('dve', lambda e: e.memset(gne[:], 64e-5), w=['gne'])
            w0T = rwp[:, 30:38].rearrange("p (j d) -> p j d", d=2)
            a0T = rwp[:, 38:46].rearrange("p (j d) -> p j d", d=2)
            ksum = al("ksum", (128, TOK), BF16)
            yT = al("yT", (128, TOK))
            gin = al("gin", (128, 5, 256), BF16)
            F = {n: al("f_" + n, (128, 256)) for n in ("lw", "P", "Q", "E", "b", "kd", "kk")}
            F["icl"] = F["Q"]
            BDs = [{n: al(f"bd_{n}{s_}", (128, 4, 128), BF16) for n in ("A", "B", "K", "BE", "KE", "V")} for s_ in range(2)]
            Rts = [al(f"Rt{s_}", (128, 256), BF16) for s_ in range(2)]
            gCs = [al(f"gC{s_}", (128, 4)) for s_ in range(2)]
            G256 = [(0, 256)] + [(256 + 256 * i, 256) for i in range(8)]
            NS = 8
            Ms = [al(f"Ms{i}", (128, 512), BF16) for i in range(NS)]
            DD = [[al(f"DD{i}{k}", (128, 256), BF16) for k in range(2)] for i in range(NS)]
            EE = [al(f"EE{i}", (128, 128), BF16) for i in range(NS)]
            lmk = al("lmk", (128, 7, 128), BF16)
            print("rwkv sbuf remaining", nc.sbuf_bytes_remaining)
            Tk = [al(f"Tk{i}", (128, 384), BF16) for i in range(NS)]
            Vx = [al(f"Vx{i}", (128, 64), BF16) for i in range(NS)]
            Wsb = [al(f"Wsb{i}", (128, 64), BF16) for i in range(2)]
            Up = [al(f"Up{i}", (128, 64), BF16) for i in range(2)]
            Ubd = [al(f"Ubd{i}", (128, 128), BF16) for i in range(2)]
            Hp = [al(f"Hp{i}", (128, 64), BF16) for i in range(2)]
            Hbd = [al(f"Hbd{i}", (128, 128), BF16) for i in range(2)]
            Hm = al("Hm", (128, 64))
            for s_ in range(2):
                for n in BDs[s_]:
                    kb.op('pool', lambda e, n=n, s_=s_: e.memset(BDs[s_][n][:], 0.0), w=[f"bd_{n}{s_}"])
            for i in range(2):
                kb.op('pool', lambda e, i=i: e.memset(Ubd[i][:], 0.0), w=[f"Ubd{i}"])
            half = (slice(0, 64), slice(64, 128))

            def v3(ap, nck):
                return ap.rearrange("p (c t) -> p c t", t=64)

            def gen(pj, d, t0, tn, gs):
                BD, Rt, gC = BDs[gs], Rts[gs], gCs[gs]
                nck = tn // 64
                for i, cidx in enumerate((pj, 4 + pj, 8 + pj, 12, 13)):
                    kb.dma('sp', gin[:, i, 0:tn], pA_d[cidx][:, t0:t0 + tn], r=[('pA', cidx)], w=[('gin', i)])
                rg, kg, vg, wlg, alg = (gin[:, i, 0:tn] for i in range(5))
                dr = slice(d * 64, (d + 1) * 64)
                pc = slice(0, 128)
                f = {n: F[n][:, 0:tn] for n in F}
                pi = nps()
                kb.op('pe', lambda e: e.matmul(ps[pi][:, 0:tn], lhsT=w2s[dr, pc], rhs=gin[dr, 3, 0:tn], start=True, stop=True),
                      r=['w2s', ('gin', 3)], w=[('ps', pi)])
                yield
                kb.op('act', lambda e: e.activation(out=f['lw'], in_=ps[pi][:, 0:tn], func=AF.Sigmoid, bias=w0T[:, pj, d:d + 1]),
                      r=[('ps', pi), 'rwp'], w=['f_lw'])
                yield
                kb.op('dve', lambda e: e.tensor_scalar(out=f['lw'], in0=f['lw'], scalar1=-0.6065306597126334, scalar2=None,
                                                       op0=ALU.mult), r=['f_lw'], w=['f_lw'])
                yield
                pi2 = nps()
                kb.op('pe', lambda e: e.matmul(ps[pi2][:, 0:tn], lhsT=a2s[dr, pc], rhs=gin[dr, 4, 0:tn], start=True, stop=True),
                      r=['a2s', ('gin', 4)], w=[('ps', pi2)])
                yield
                kb.op('act', lambda e: e.activation(out=f['icl'], in_=ps[pi2][:, 0:tn], func=AF.Sigmoid, bias=a0T[:, pj, d:d + 1]),
                      r=[('ps', pi2), 'rwp'], w=['f_Q'])
                yield
                kb.op('dve', lambda e: e.tensor_scalar(out=f['kk'], in0=kg, scalar1=rwp[:, 46 + pj:47 + pj], scalar2=None,
                                                       op0=ALU.mult), r=[('gin', 1), 'rwp'], w=['f_kk'])
                yield
                kb.op('pool', lambda e: e.tensor_tensor(out=f['E'], in0=f['kk'], in1=f['kk'], op=ALU.mult), r=['f_kk'], w=['f_E'])
                yield
                pi3 = nps()
                kb.op('pe', lambda e: e.matmul(ps[pi3][:, 0:tn], lhsT=blkf[:, :], rhs=f['E'], start=True, stop=True),
                      r=['blkf', 'f_E'], w=[('ps', pi3)])
                yield
                kb.op('act', lambda e: e.activation(out=f['E'], in_=ps[pi3][:, 0:tn], func=AF.Sqrt), r=[('ps', pi3)], w=['f_E'])
                yield
                kb.op('dve', lambda e: e.tensor_scalar(out=f['E'], in0=f['E'], scalar1=1e-12, scalar2=None, op0=ALU.max),
                      r=['f_E'], w=['f_E'])
                yield
                kb.op('dve', lambda e: e.reciprocal(out=f['E'], in_=f['E']), r=['f_E'], w=['f_E'])
                yield
                kb.op('dve', lambda e: e.tensor_tensor(out=f['kk'], in0=f['kk'], in1=f['E'], op=ALU.mult), r=['f_kk', 'f_E'], w=['f_kk'])
                yield
                kb.op('dve', lambda e: e.tensor_tensor(out=f['b'], in0=f['kk'], in1=f['icl'], op=ALU.mult),
                      r=['f_kk', 'f_Q'], w=['f_b'])
                yield
                kb.op('dve', lambda e: e.tensor_scalar(out=f['kd'], in0=f['icl'], scalar1=-1.0, scalar2=rwp[:, 50 + pj:51 + pj],
                                                       op0=ALU.add, op1=ALU.mult), r=['f_Q', 'rwp'], w=['f_kd'])
                yield
                kb.op('dve', lambda e: e.scalar_tensor_tensor(out=f['kd'], in0=f['kd'], scalar=1.0, in1=kg, op0=ALU.add, op1=ALU.mult),
                      r=['f_kd', ('gin', 1)], w=['f_kd'])
                yield
                if d == 0:
                    kb.op('pool', lambda e: e.tensor_copy(out=ksum[:, t0:t0 + tn], in_=f['kd']), r=['f_kd'], w=['ksum'])
                else:
                    kb.op('pool', lambda e: e.tensor_tensor(out=ksum[:, t0:t0 + tn], in0=ksum[:, t0:t0 + tn], in1=f['kd'], op=ALU.add),
                          r=['f_kd', 'ksum'], w=['ksum'])
                kb.op('dve', lambda e: e.tensor_tensor_scan(out=f['P'], data0=mscan[:, 0:tn], data1=f['lw'], initial=0.0,
                                                            op0=ALU.mult, op1=ALU.add), r=['mscan', 'f_lw'], w=['f_P'])
                yield
                tot = v3(f['P'], nck)[:, :, 63:64]
                kb.op('dve', lambda e: e.tensor_tensor(out=v3(f['Q'], nck), in0=tot.to_broadcast([128, nck, 64]), in1=v3(f['P'], nck),
                                                       op=ALU.subtract), r=['f_P'], w=['f_Q'])
                yield
                kb.op('act', lambda e: e.activation(out=gC[:, 0:nck], in_=tot.rearrange("p c o -> p (c o)"), func=AF.Exp),
                      r=['f_P'], w=[f'gC{gs}'])
                yield
                if d == 0:
                    kb.op('dve', lambda e: e.tensor_tensor(out=f['lw'], in0=f['P'], in1=f['lw'], op=ALU.subtract), r=['f_P', 'f_lw'], w=['f_lw'])
                    Lex, Lc, rem = f['lw'], f['P'], f['Q']
                    kx, kc_, kr_ = 'f_lw', 'f_P', 'f_Q'
                else:
                    kb.op('dve', lambda e: e.tensor_tensor(out=f['P'], in0=f['P'], in1=f['lw'], op=ALU.subtract), r=['f_P', 'f_lw'], w=['f_P'])
                    kb.op('dve', lambda e: e.tensor_tensor(out=f['lw'], in0=f['Q'], in1=f['lw'], op=ALU.add), r=['f_Q', 'f_lw'], w=['f_lw'])
                    Lex, Lc, rem = f['Q'], f['lw'], f['P']
                    kx, kc_, kr_ = 'f_Q', 'f_lw', 'f_P'

                def bd_write(name, src, srckey, neg=False):
                    for hh in range(2):
                        o = BD[name][half[hh], 0:nck, hh * 64:(hh + 1) * 64]
                        kb.op('dve' if hh == 0 else 'pool', lambda e, o=o, hh=hh: e.tensor_tensor(
                            out=o, in0=v3(src[half[hh], :], nck), in1=v3(f['E'][half[hh], :], nck), op=ALU.mult),
                            r=[srckey, 'f_E'], w=[f'bd_{name}{gs}'])
                kb.op('dve', lambda e: e.tensor_scalar(out=f['kk'], in0=f['kk'], scalar1=-1.0, scalar2=None, op0=ALU.mult),
                      r=['f_kk'], w=['f_kk'])
                yield
                kb.op('act', lambda e: e.activation(out=f['E'], in_=Lex, func=AF.Exp), r=[kx], w=['f_E'])
                yield
                bd_write('A', f['kk'], 'f_kk', neg=True)
                yield
                kb.op('act', lambda e: e.activation(out=f['E'], in_=Lc, func=AF.Exp), r=[kc_], w=['f_E'])
                yield
                kb.op('dve', lambda e: e.tensor_tensor(out=Rt[:, 0:tn], in0=rg, in1=f['E'], op=ALU.mult), r=[('gin', 0), 'f_E'], w=[f'Rt{gs}'])
                yield
                kb.op('act', lambda e: e.activation(out=f['E'], in_=Lc, func=AF.Exp, scale=-1.0), r=[kc_], w=['f_E'])
                yield
                bd_write('B', f['b'], 'f_b')
                yield
                bd_write('K', f['kd'], 'f_kd')
                yield
                kb.op('act', lambda e: e.activation(out=f['E'], in_=rem, func=AF.Exp), r=[kr_], w=['f_E'])
                yield
                bd_write('BE', f['b'], 'f_b')
                yield
                bd_write('KE', f['kd'], 'f_kd')
                yield
                for hh in range(2):
                    kb.op('pool', lambda e, hh=hh: e.tensor_copy(out=BD['V'][half[hh], 0:nck, hh * 64:(hh + 1) * 64],
                                                                in_=v3(gin[half[hh], 2, 0:tn], nck)), r=[('gin', 2)], w=[f'bd_V{gs}'])

            CH = {'n': 0}

            TTS = {}

            def chunk_off(d, ci, cs, gs):
                BD, Rt = BDs[gs], Rts[gs]
                ms, tk = Ms[cs], Tk[cs]
                kms, ktk = f"Ms{cs}", f"Tk{cs}"
                Axc, Bxc, Kxc = BD['A'][:, ci, :], BD['B'][:, ci, :], BD['K'][:, ci, :]
                Rtc = Rt[:, ci * 64:(ci + 1) * 64]
                pi = nps()
                P_ = ps[pi]
                kb.op('pe', lambda e: e.matmul(P_[:, 0:128], lhsT=Bxc, rhs=Axc, start=True, stop=True), r=[f'bd_B{gs}', f'bd_A{gs}'], w=[('ps', pi)], inc=False)
                kb.op('pe', lambda e: e.matmul(P_[:, 128:192], lhsT=Bxc, rhs=Rtc, start=True, stop=True), r=[f'bd_B{gs}', f'Rt{gs}'], w=[('ps', pi)], inc=False)
                kb.op('pe', lambda e: e.matmul(P_[:, 192:320], lhsT=Kxc, rhs=Axc, start=True, stop=True), r=[f'bd_K{gs}', f'bd_A{gs}'], w=[('ps', pi)], inc=False)
                kb.op('pe', lambda e: e.matmul(P_[:, 320:384], lhsT=Kxc, rhs=Rtc, start=True, stop=True), r=[f'bd_K{gs}', f'Rt{gs}'], w=[('ps', pi)], inc=False)
                kb.op('pe', lambda e: e.matmul(P_[:, 384:512], lhsT=Axc, rhs=Bxc, start=True, stop=True), r=[f'bd_B{gs}', f'bd_A{gs}'], w=[('ps', pi)])
                kb.op('dve', lambda e: e.tensor_tensor(out=ms[:], in0=P_[:, :], in1=mall[:, :], op=ALU.mult),
                      r=[('ps', pi), 'mall'], w=[kms])
                yield
                pi = nps()
                pb = ps[pi][:].bitcast(BF16)
                for i, n in enumerate(('BE', 'KE', 'V')):
                    kb.op('pe', lambda e, i=i, n=n: e.transpose(pb[:, i * 128:(i + 1) * 128], BD[n][:, ci, :], identb[:]),
                          r=[f'bd_{n}{gs}', 'identb'], w=[('ps', pi)], inc=(i == 2))
                kb.op('act', lambda e: e.copy(out=tk[:], in_=pb[:, 0:384]), r=[('ps', pi)], w=[ktk])
                for hh in range(2):
                    kb.op('act', lambda e, hh=hh: e.copy(out=Vx[cs][half[hh], :], in_=tk[half[hh], 256 + hh * 64:320 + hh * 64]),
                          r=[ktk], w=[f"Vx{cs}"])
                dd = DD[cs]
                di = 0
                kdd = lambda i: f"DD{cs}{i}"
                kb.op('pool', lambda e: e.tensor_tensor(out=dd[0][:, 128:256], in0=ms[:, 0:128], in1=lmk[:, 0, :], op=ALU.mult),
                      r=[kms, 'lmk'], w=[kdd(0)])
                kb.op('pool', lambda e: e.tensor_tensor(out=dd[0][:, 128:256], in0=dd[0][:, 128:256], in1=identb[:], op=ALU.add),
                      r=[kdd(0), 'identb'], w=[kdd(0)])
                kb.op('pool', lambda e: e.tensor_tensor(out=dd[0][:, 0:128], in0=ms[:, 384:512], in1=lmk[:, 6, :], op=ALU.mult),
                      r=[kms, 'lmk'], w=[kdd(0)])
                kb.op('pool', lambda e: e.tensor_tensor(out=dd[0][:, 0:128], in0=dd[0][:, 0:128], in1=identb[:], op=ALU.add),
                      r=[kdd(0), 'identb'], w=[kdd(0)])
                yield
                for lv in range(1, 6):
                    pi = nps()
                    kb.op('pe', lambda e, lv=lv, pi=pi, di=di: e.matmul(ps[pi][:, 0:128], lhsT=ms[:, 0:128], rhs=dd[di][:, 0:128],
                                                                    start=True, stop=True), r=[kms, kdd(di)], w=[('ps', pi)])
                    ee = EE[cs]
                    kee = f"EE{cs}"
                    kb.op('dve', lambda e, pi=pi, lv=lv: e.tensor_tensor(out=ee[:], in0=ps[pi][:, 0:128], in1=lmk[:, lv, :], op=ALU.mult),
                          r=[('ps', pi), 'lmk'], w=[kee])
                    yield
                    pi2 = nps()
                    if lv < 5:
                        kb.op('pe', lambda e, pi2=pi2, di=di: e.matmul(ps[pi2][:, 0:128], lhsT=dd[di][:, 128:256], rhs=ee[:],
                                                                      start=True, stop=True), r=[kee, kdd(di)], w=[('ps', pi2)], inc=False)
                    kb.op('pe', lambda e, pi2=pi2, di=di: e.matmul(ps[pi2][:, 128:256], lhsT=ee[:], rhs=dd[di][:, 128:256],
                                                                  start=True, stop=True), r=[kee, kdd(di)], w=[('ps', pi2)])
                    lo = 0 if lv < 5 else 128
                    kb.op('dve', lambda e, pi2=pi2, di=di, lo=lo: e.tensor_tensor(out=dd[1 - di][:, lo:256], in0=ps[pi2][:, lo:256],
                                                                                 in1=dd[di][:, lo:256], op=ALU.add),
                          r=[('ps', pi2), kdd(di)], w=[kdd(1 - di)])
                    di = 1 - di
                    yield
                TTS[cs] = (dd[di][:, 128:256], kdd(di))

            def chunk_on(d, ci, cs, gs, tcol):
                BD, Rt, gC = BDs[gs], Rts[gs], gCs[gs]
                ms, tk = Ms[cs], Tk[cs]
                kms, ktk = f"Ms{cs}", f"Tk{cs}"
                Axc = BD['A'][:, ci, :]
                Rtc = Rt[:, ci * 64:(ci + 1) * 64]
                TT, kTT = TTS[cs]
                ob = CH['n'] % 2
                CH['n'] += 1
                hc = CH['h']
                hn = 1 - hc
                pi = nps()
                kb.op('pe', lambda e: e.matmul(ps[pi][:, 0:64], lhsT=ms[:, 192:320], rhs=Vx[cs][:], start=True, stop=False),
                      r=[kms, f"Vx{cs}"], w=[('ps', pi)])
                kb.op('pe', lambda e: e.matmul(ps[pi][:, 0:64], lhsT=Axc, rhs=Hp[hc][:], start=False, stop=True),
                      r=[f'bd_A{gs}', f"Hp{hc}"], w=[('ps', pi)])
                kb.op('act', lambda e: e.copy(out=Wsb[ob][:], in_=ps[pi][:, 0:64]), r=[('ps', pi)], w=[f"Wsb{ob}"])
                yield
                pi = nps()
                kb.op('pe', lambda e: e.matmul(ps[pi][:, 0:64], lhsT=TT, rhs=Wsb[ob][:], start=True, stop=True),
                      r=[kTT, f"Wsb{ob}"], w=[('ps', pi)])
                kb.op('dve', lambda e: e.tensor_copy(out=Up[ob][:], in_=ps[pi][:, 0:64]), r=[('ps', pi)], w=[f"Up{ob}"])
                for hh in range(2):
                    kb.op('act' if hh == 0 else 'pool', lambda e, hh=hh: (e.copy if hh == 0 else e.tensor_copy)(
                        out=Ubd[ob][half[hh], hh * 64:(hh + 1) * 64], in_=(ps[pi][half[hh], 0:64] if hh == 0 else Up[ob][half[hh], :])),
                        r=[('ps', pi), f"Up{ob}"], w=[f"Ubd{ob}"])
                yield
                piy = nps()
                kb.op('pe', lambda e: e.matmul(ps[piy][:, 0:64], lhsT=Hbd[hc][:], rhs=Rtc, start=True, stop=False),
                      r=[f"Hbd{hc}", f'Rt{gs}'], w=[('ps', piy)])
                kb.op('pe', lambda e: e.matmul(ps[piy][:, 0:64], lhsT=Ubd[ob][:], rhs=ms[:, 128:192], start=False, stop=False),
                      r=[f"Ubd{ob}", kms], w=[('ps', piy)])
                kb.op('pe', lambda e: e.matmul(ps[piy][:, 0:64], lhsT=tk[:, 256:384], rhs=ms[:, 320:384], start=False, stop=True),
                      r=[ktk, kms], w=[('ps', piy)])
                ysl = yT[:, tcol:tcol + 64]
                if d == 0:
                    kb.op('act', lambda e: e.copy(out=ysl, in_=ps[piy][:, 0:64]), r=[('ps', piy)], w=['yT'])
                else:
                    kb.op('dve', lambda e: e.tensor_tensor(out=ysl, in0=ps[piy][:, 0:64], in1=ysl, op=ALU.add), r=[('ps', piy), 'yT'], w=['yT'])
                yield
                pih = nps()
                kb.op('pe', lambda e: e.matmul(ps[pih][:, 0:64], lhsT=tk[:, 0:128], rhs=Up[ob][:], start=True, stop=False),
                      r=[ktk, f"Up{ob}"], w=[('ps', pih)])
                kb.op('pe', lambda e: e.matmul(ps[pih][:, 0:64], lhsT=tk[:, 128:256], rhs=Vx[cs][:], start=False, stop=True),
                      r=[ktk, f"Vx{cs}"], w=[('ps', pih)])
                kb.op('dve', lambda e: e.scalar_tensor_tensor(out=Hm[:], in0=Hm[:], scalar=gC[:, ci:ci + 1], in1=ps[pih][:, 0:64],
                                                              op0=ALU.mult, op1=ALU.add), r=['Hm', f'gC{gs}', ('ps', pih)], w=['Hm'])
                kb.op('act', lambda e: e.copy(out=Hp[hn][:], in_=Hm[:]), r=['Hm'], w=[f"Hp{hn}"])
                for hh in range(2):
                    kb.op('act', lambda e, hh=hh: e.copy(out=Hbd[hn][half[hh], hh * 64:(hh + 1) * 64], in_=Hm[half[hh], :]),
                          r=['Hm'], w=[f"Hbd{hn}"])
                CH['h'] = hn
                yield

            def out_stage(pj, t0, tn):
                Rt = Rts[0]
                for i, cidx in enumerate((pj, 8 + pj, 14)):
                    kb.dma('sp', gin[:, i, 0:tn], pA_d[cidx][:, t0:t0 + tn], r=[('pA', cidx)], w=[('gin', i)])
                rg, vg, glg = (gin[:, i, 0:tn] for i in range(3))
                f = {n: F[n][:, 0:tn] for n in F}
                ysl = yT[:, t0:t0 + tn]
                pi = nps()
                kb.op('pe', lambda e: e.matmul(ps[pi][:, 0:tn], lhsT=blkf[:, :], rhs=ysl, start=True, stop=True), r=['blkf', 'yT'], w=[('ps', pi)])
                kb.op('dve', lambda e: e.scalar_tensor_tensor(out=f['P'], in0=ps[pi][:, 0:tn], scalar=-1.0 / 64, in1=ysl,
                                                              op0=ALU.mult, op1=ALU.add), r=[('ps', pi), 'yT'], w=['f_P'])
                kb.op('pool', lambda e: e.tensor_tensor(out=f['Q'], in0=f['P'], in1=f['P'], op=ALU.mult), r=['f_P'], w=['f_Q'])
                pi = nps()
                kb.op('pe', lambda e: e.matmul(ps[pi][:, 0:tn], lhsT=blkf[:, :], rhs=f['Q'], start=True, stop=True), r=['blkf', 'f_Q'], w=[('ps', pi)])
                kb.op('act', lambda e: e.activation(out=f['E'], in_=ps[pi][:, 0:tn], func=AF.Sqrt, bias=gne[:, 0:1], scale=1.0 / 64),
                      r=[('ps', pi), 'gne'], w=['f_E'])
                kb.op('dve', lambda e: e.reciprocal(out=f['E'], in_=f['E']), r=['f_E'], w=['f_E'])
                kb.op('dve', lambda e: e.tensor_tensor(out=f['P'], in0=f['P'], in1=f['E'], op=ALU.mult), r=['f_P', 'f_E'], w=['f_P'])
                kb.op('dve', lambda e: e.tensor_scalar(out=f['P'], in0=f['P'], scalar1=rwp[:, 58 + pj:59 + pj], scalar2=rwp[:, 62 + pj:63 + pj],
                                                       op0=ALU.mult, op1=ALU.add), r=['f_P', 'rwp'], w=['f_P'])
                kb.op('pool', lambda e: e.tensor_tensor(out=Rt[:, 0:tn], in0=rg, in1=ksum[:, t0:t0 + tn], op=ALU.mult),
                      r=[('gin', 0), 'ksum'], w=['Rt0'])
                pi = nps()
                kb.op('pe', lambda e: e.matmul(ps[pi][:, 0:tn], lhsT=blkRK[:, :], rhs=Rt[:, 0:tn], start=True, stop=True),
                      r=['blkRK', 'Rt0'], w=[('ps', pi)])
                kb.op('dve', lambda e: e.tensor_tensor(out=f['Q'], in0=ps[pi][:, 0:tn], in1=vg, op=ALU.mult), r=[('ps', pi), ('gin', 1)], w=['f_Q'])
                kb.op('dve', lambda e: e.tensor_tensor(out=f['P'], in0=f['P'], in1=f['Q'], op=ALU.add), r=['f_P', 'f_Q'], w=['f_P'])
                pi = nps()
                kb.op('pe', lambda e: e.matmul(ps[pi][:, 0:tn], lhsT=g2s[:, :], rhs=glg, start=True, stop=True),
                      r=['g2s', ('gin', 2)], w=[('ps', pi)])
                kb.op('dve', lambda e: e.tensor_tensor(out=mixT[:, pj, t0:t0 + tn], in0=ps[pi][:, 0:tn], in1=f['P'], op=ALU.mult),
                      r=[('ps', pi), 'f_P'], w=[('mixT', pj)])

            for pj in range(4):
                pcs = slice(pj * 128, (pj + 1) * 128)
                kb.dma('pool', w2s[:], w2_d[j][:, pcs], w=['w2s'])
                kb.dma('pool', a2s[:], a2_d[j][:, pcs], w=['a2s'])
                kb.dma('pool', g2s[:], g2_d[j][:, pcs], w=['g2s'])
                kb.op('dve', lambda e: e.tensor_scalar(out=blkRK[:], in0=blkf[:], scalar1=rwp[:, 54 + pj:55 + pj], scalar2=None,
                                                       op0=ALU.mult), r=['blkf', 'rwp'], w=['blkRK'])
                for d in range(2):
                    kb.dma('pool', mall[:], mall_d[:, d, :], w=['mall'])
                    kb.dma('pool', lmk[:], lmk_d[:, d, :, :], w=['lmk'])
                    CH['h'] = 0
                    kb.op('dve', lambda e: e.memset(Hm[:], 0.0), w=['Hm'])
                    kb.op('dve', lambda e: e.memset(Hp[0][:], 0.0), w=['Hp0'])
                    for i in range(2):
                        kb.op('pool', lambda e, i=i: e.memset(Hbd[i][:], 0.0), w=[f"Hbd{i}"])
                    order = G256 if d == 0 else [G256[0]] + G256[:0:-1]

                    def drain(gens):
                        while gens:
                            for g in list(gens):
                                try:
                                    next(g)
                                except StopIteration:
                                    gens.remove(g)
                    def offs(gi):
                        cis = list(range(4) if d == 0 else range(3, -1, -1))
                        return [chunk_off(d, ci, (gi % 2) * 4 + k, gi % 2) for k, ci in enumerate(cis)]

                    def on_then_gen(gi):
                        t0 = order[gi][0]
                        cis = list(range(4) if d == 0 else range(3, -1, -1))
                        for k, ci in enumerate(cis):
                            yield from chunk_on(d, ci, (gi % 2) * 4 + k, gi % 2, t0 + ci * 64)
                        if gi + 2 < len(order):
                            yield from gen(pj, d, order[gi + 2][0], order[gi + 2][1], gi % 2)
                    drain([gen(pj, d, order[0][0], order[0][1], 0)])
                    drain(offs(0) + [gen(pj, d, order[1][0], order[1][1], 1)])
                    for gi in range(len(order)):
                        gens = [on_then_gen(gi)]
                        if gi + 1 < len(order):
                            gens += offs(gi + 1)
                        drain(gens)
                for (t0, tn) in G256:
                    out_stage(pj, t0, tn)
            kb.barrier(keys_all + [('gin', i) for i in range(5)])

        def even_mixer(L, mixT):
            with uT_scope():
                prep(0)
                with contextlib.ExitStack() as es2:
                    even_B(L, mixT, es2)
                if rwkv:
                    with contextlib.ExitStack() as es2:
                        rwkv_project(L, es2)
            if rwkv:
                with contextlib.ExitStack() as es2:
                    rwkv_passes(L, mixT, es2)
            else:
                for c in range(4):
                    kb.op('pool', lambda e, c=c: e.memset(mixT[:, c, :], 0.0), w=[('mixT', c)])

        def even_B(L, mixT, es2):
            j = L // 2
            sb2 = lambda name, shape, dt=F32: es2.enter_context(SBT(name, list(shape), dt))
            wsrc = wev_d[j]
            wb = {'n': 0, 'bufs': [sb2(f"wb{i}", (128, 8, 128), BF16) for i in range(2)]}
            A = {'g': 0, 's': 0, 'p': 0, 'PT': [sb2(f"PT{i}", (128, 512), BF16) for i in range(3)],
                 'rden': sb2("rden", (128, 512))}
            with contextlib.ExitStack() as es3:
                sb3 = lambda name, shape, dt=F32: es3.enter_context(SBT(name, list(shape), dt))
                kA = sb3("kA", (128, TOK), BF16)
                vt = sb3("vt", (128, NT, 192), BF16)
                kb.op('pool', lambda e: e.memset(vt[:, :, 64:128], 1.0), w=['vt'])
                wmask = sb3("wmask", (128, 6, 512), BF16)
                RP['C'] = sb3("ropeC", (128, TOK), BF16)
                RP['S'] = sb3("ropeS", (128, TOK), BF16)
                kb.dma('pool', RP['C'][:], ropeC_d, w=['rope'])
                kb.dma('pool', RP['S'][:], ropeS_d, w=['rope'])
                esk = sb3("esk", (1, 8))
                eskb = sb3("eskb", (128, 8))
                qa = sb3("qa", (128, TOK), BF16)
                qp = sb3("qp", (128, TOK), BF16)
                kb.dma('pool', wmask[:], wmask_d, w=['wmask'])
                kb.dma('sp', esk[:], sink_d[j:j + 1, :], w=['esk'])
                kb.op('act', lambda e: e.activation(out=esk[:], in_=esk[:], func=AF.Exp), r=['esk'], w=['esk'])
                pi0 = nps()
                kb.op('pe', lambda e: e.matmul(ps[pi0][:, 0:8], lhsT=ones_f[0:1, :], rhs=esk[0:1, :], start=True, stop=True),
                      r=['ones_f', 'esk'], w=[('ps', pi0)])
                kb.op('dve', lambda e: e.tensor_copy(out=eskb[:], in_=ps[pi0][:, 0:8]), r=[('ps', pi0)], w=['eskb'])

                def evx(dst, key):
                    def f(c, t0, tn, pi):
                        kb.op('dve', lambda e: e.tensor_copy(out=dst[:, t0:t0 + tn], in_=ps[pi][:, 0:tn]),
                              r=[('ps', pi)], w=[key])
                    return f
                proj_fm(wsrc[:, 23 * 128:24 * 128], 1, wb, evx(kA, ('kx', 0)))
                proj_fm(wsrc[:, 24 * 128:25 * 128], 1, wb, evx(qp, ('qx', 1)))
                rope(kA, qp, ('kx', 0), ('qx', 1))

                def evv(t, pi):
                    kb.op('act', lambda e: e.copy(out=vt[:, t, 0:64], in_=ps[pi][:, 0:64]), r=[('ps', pi)], w=['vt'])
                    kb.op('dve', lambda e: e.tensor_copy(out=vt[:, t, 128:192], in_=ps[pi][:, 64:128]), r=[('ps', pi)], w=['vt'])
                proj_tm(wsrc[:, 25 * 128:26 * 128], 128, wb, evv)

                for qc in range(4):
                    def evq(c, t0, tn, pi):
                        dst = (qa, qp)[c]
                        kb.op('dve' if c == 0 else 'act', lambda e: (e.tensor_copy if c == 0 else e.copy)(
                            out=dst[:, t0:t0 + tn], in_=ps[pi][:, 0:tn]), r=[('ps', pi)], w=[('qx', c)])
                    proj_fm(wsrc[:, (15 + qc) * 128:(16 + qc) * 128], 1, wb, lambda c, t0, tn, pi: evq(0, t0, tn, pi))
                    proj_fm(wsrc[:, (19 + qc) * 128:(20 + qc) * 128], 1, wb, lambda c, t0, tn, pi: evq(1, t0, tn, pi))
                    rope(qa, qp, ('qx', 0), ('qx', 1))
                    for hh in range(2):
                        hd = qc + 4 * hh
                        base = hh * 64
                        mrow = slice(base, base + 64)
                        attn_group(A, qa, ('qx', 0), base, kA, ('kx', 0), vt, 'vt', 0, 256, [(0, None), (1, None)],
                                   mixT[mrow, 4 + qc, 0:256], ('mixT', 4 + qc), seed=eskb[:, hd:hd + 1])
                        for Gq in range(4):
                            keys = [(0, None), (1, None)]
                            for dl in range(-1, 5):
                                lt = 4 * Gq + dl
                                if 0 <= lt < 16:
                                    c0, c1 = max(0, dl - 1) * 128, min(4, dl + 2) * 128
                                    keys.append((2 + lt, (wmask[:, dl + 1, c0:c1], 'wmask'), (c0, c1)))
                            q0 = 256 + 512 * Gq
                            attn_group(A, qa, ('qx', 0), base, kA, ('kx', 0), vt, 'vt', q0, 512, keys,
                                       mixT[mrow, 4 + qc, q0:q0 + 512], ('mixT', 4 + qc), seed=eskb[:, hd:hd + 1])
                kb.barrier([('kx', 0), 'vt', 'wmask', 'esk', 'eskb', ('qx', 0), ('qx', 1), 'rope'])
            kb.barrier([('wb', 0), ('wb', 1), ('PT', 0), ('PT', 1), ('PT', 2), 'rden'])


        def odd_mixer(L, mixT):
            j = L // 2
            wsrc = wod_d[j]
            with uT_scope():
                prep(0)
                with contextlib.ExitStack() as es2:
                    sb2 = lambda name, shape, dt=F32: es2.enter_context(SBT(name, list(shape), dt))
                    wb = {'n': 0, 'bufs': [sb2(f"wb{i}", (128, 8, 128), BF16) for i in range(2)]}
                    A = {'g': 0, 's': 0, 'p': 0, 'PT': [sb2(f"PT{i}", (128, 512), BF16) for i in range(3)],
                         'rden': sb2("rden", (128, 512))}
                    qa = sb2("qa", (128, TOK), BF16)

                    def evx(dst, key, eng='dve'):
                        def f(c, t0, tn, pi):
                            kb.op(eng, lambda e: (e.tensor_copy if eng == 'dve' else e.copy)(out=dst[:, t0:t0 + tn], in_=ps[pi][:, 0:tn]),
                                  r=[('ps', pi)], w=[key])
                        return f
                    with contextlib.ExitStack() as es3:
                        sb3 = lambda name, shape, dt=F32: es3.enter_context(SBT(name, list(shape), dt))
                        vtc = sb3("vtc", (128, NT, 192), BF16)
                        kb.op('pool', lambda e: e.memset(vtc[:, :, 64:128], 1.0), w=['vtc'])
                        kc_ = sb3("kc", (128, TOK), BF16)
                        bst = sb3("bst", (128, (NCT + 1) * 128))
                        Et = sb3("Et", (128, NCT + 1, 128), BF16)
                        for jp in range(4):
                            proj_fm(wsrc[:, jp * 128:(jp + 1) * 128], 1, wb, evx(qa, ('qx', 0)))
                            proj_fm(wsrc[:, (4 + jp) * 128:(5 + jp) * 128], 1, wb, evx(kc_, ('kx', 0), 'act'))

                            def evv(t, pi):
                                kb.op('dve', lambda e: e.tensor_copy(out=vtc[:, t, 0:64], in_=ps[pi][:, 0:64]), r=[('ps', pi)], w=['vtc'])
                                kb.op('act', lambda e: e.copy(out=vtc[:, t, 128:192], in_=ps[pi][:, 64:128]), r=[('ps', pi)], w=['vtc'])
                            proj_tm(wsrc[:, (8 + jp) * 128:(9 + jp) * 128], 128, wb, evv)
                            for hh in range(2):
                                hd = jp * 2 + hh
                                base = hh * 64
                                mrow = slice(base, base + 64)
                                kb.dma('sp', bst[:], cbias_d[j, hd], w=['bst'])
                                kb.op('act', lambda e: e.activation(out=Et[:].rearrange("p a b -> p (a b)"), in_=bst[:], func=AF.Exp),
                                      r=['bst'], w=['Et'])
                                attn_group(A, qa, ('qx', 0), base, kc_, ('kx', 0), vtc, 'vtc', 0, 256, [(0, None), (1, None)],
                                           mixT[mrow, jp, 0:256], ('mixT', jp))
                                for m in range(0, 16, 2):
                                    keys = [(0, None), (1, None)]
                                    ta = {m + dl: ti for (dl, ti) in C_TILES[c_class(m)]}
                                    tb = {m + 1 + dl: ti for (dl, ti) in C_TILES[c_class(m + 1)]}
                                    for kt in sorted(set(ta) | set(tb)):
                                        ia, ib = ta.get(kt), tb.get(kt)
                                        pieces = []
                                        if ia is not None:
                                            pieces.append((0, 128, Et[:, ia, :]))
                                        if ib is not None:
                                            pieces.append((128, 256, Et[:, ib, :]))
                                        keys.append((2 + kt, (pieces, 'Et'), (0 if ia is not None else 128, 256 if ib is not None else 128)))
                                    q0 = 256 + 128 * m
                                    attn_group(A, qa, ('qx', 0), base, kc_, ('kx', 0), vtc, 'vtc', q0, 256, keys,
                                               mixT[mrow, jp, q0:q0 + 256], ('mixT', jp))
                        kb.barrier(['vtc', 'bst', 'Et', ('kx', 0)])
                    with contextlib.ExitStack() as es3:
                        sb3 = lambda name, shape, dt=F32: es3.enter_context(SBT(name, list(shape), dt))
                        kA = sb3("kA", (128, TOK), BF16)
                        vt = sb3("vt", (128, NT, 192), BF16)
                        kb.op('pool', lambda e: e.memset(vt[:, :, 64:128], 1.0), w=['vt'])
                        qp = sb3("qp", (128, TOK), BF16)
                        RP['C'] = sb3("ropeC", (128, TOK), BF16)
                        RP['S'] = sb3("ropeS", (128, TOK), BF16)
                        blkf = sb3("blkf", (128, 128))
                        dg = sb3("dg", (128, 4))
                        sq = sb3("sq", (128, 512))
                        rs = sb3("rs", (128, 512))
                        kb.dma('pool', RP['C'][:], ropeC_d, w=['rope'])
                        kb.dma('pool', RP['S'][:], ropeS_d, w=['rope'])
                        kb.dma('sp', blkf[:], blkf_d, w=['blkf'])
                        kb.dma('sp', dg[:], dgain_d[j], w=['dg'])

                        def evn(dst, key, gcol):
                            def f(c, t0, tn, pi):
                                kb.op('act', lambda e: e.activation(out=sq[:, 0:tn], in_=ps[pi][:, 0:tn], func=AF.Square),
                                      r=[('ps', pi)], w=['sq'])
                                pi2 = nps()
                                kb.op('pe', lambda e: e.matmul(ps[pi2][:, 0:tn], lhsT=blkf[:, :], rhs=sq[:, 0:tn], start=True, stop=True),
                                      r=['blkf', 'sq'], w=[('ps', pi2)])
                                kb.op('act', lambda e: e.activation(out=rs[:, 0:tn], in_=ps[pi2][:, 0:tn], func=AF.Sqrt,
                                                                    bias=epsT[:, 0:1], scale=1.0 / 64), r=[('ps', pi2), 'epsT'], w=['rs'])
                                kb.op('dve', lambda e: e.reciprocal(out=rs[:, 0:tn], in_=rs[:, 0:tn]), r=['rs'], w=['rs'])
                                kb.op('dve', lambda e: e.scalar_tensor_tensor(out=dst[:, t0:t0 + tn], in0=ps[pi][:, 0:tn],
                                                                              scalar=dg[:, gcol:gcol + 1], in1=rs[:, 0:tn],
                                                                              op0=ALU.mult, op1=ALU.mult),
                                      r=[('ps', pi), 'dg', 'rs'], w=[key])
                            return f
                        proj_fm(wsrc[:, 20 * 128:21 * 128], 1, wb, evn(kA, ('kx', 1), 2))
                        proj_fm(wsrc[:, 21 * 128:22 * 128], 1, wb, evn(qp, ('qx', 1), 3))
                        rope(kA, qp, ('kx', 1), ('qx', 1))

                        def evv2(t, pi):
                            kb.op('act', lambda e: e.copy(out=vt[:, t, 0:64], in_=ps[pi][:, 0:64]), r=[('ps', pi)], w=['vt'])
                            kb.op('dve', lambda e: e.tensor_copy(out=vt[:, t, 128:192], in_=ps[pi][:, 64:128]), r=[('ps', pi)], w=['vt'])
                        proj_tm(wsrc[:, 22 * 128:23 * 128], 128, wb, evv2)
                        for qc in range(4):
                            proj_fm(wsrc[:, (12 + qc) * 128:(13 + qc) * 128], 1, wb, evn(qa, ('qx', 0), 0))
                            proj_fm(wsrc[:, (16 + qc) * 128:(17 + qc) * 128], 1, wb, evn(qp, ('qx', 1), 1))
                            rope(qa, qp, ('qx', 0), ('qx', 1))
                            for hh in range(2):
                                base = hh * 64
                                mrow = slice(base, base + 64)
                                attn_group(A, qa, ('qx', 0), base, kA, ('kx', 1), vt, 'vt', 0, 256, [(0, None), (1, None)],
                                           mixT[mrow, 4 + qc, 0:256], ('mixT', 4 + qc))
                                for Gq in range(4):
                                    q0 = 256 + 512 * Gq
                                    attn_group(A, qa, ('qx', 0), base, kA, ('kx', 1), vt, 'vt', q0, 512,
                                               [(t, None) for t in range(NT)], mixT[mrow, 4 + qc, q0:q0 + 512], ('mixT', 4 + qc))
                        kb.barrier([('kx', 1), 'vt', ('qx', 1), 'rope', 'blkf', 'dg', 'sq', 'rs'])
                    kb.barrier([('wb', 0), ('wb', 1), ('PT', 0), ('PT', 1), ('PT', 2), 'rden', ('qx', 0), ('kx', 0)])

        def mixer(L):
            with contextlib.ExitStack() as es2:
                mixT = es2.enter_context(SBT("mixT", [128, 8, TOK], BF16))
                if L % 2 == 0:
                    even_mixer(L, mixT)
                else:
                    odd_mixer(L, mixT)
                if dbg == 'mixT' and L == 0 or (dbg == 'mixT1' and L == 1) or (dbg == 'mixT2' and L == 2) or (dbg == 'mixT3' and L == 3):
                    for c in range(8):
                        kb.dma('sp', dbg_d[:, c, :], mixT[:, c, :], r=[('mixT', c)], w=[('dbg', c)])
                outproj(L, mixT)
                kb.barrier([('mixT', c) for c in range(8)])

        for L in range(nlayers):
            adaln(L)
            ffn_convert(L)
            if mix:
                mixer(L)
            with uT_scope():
                prep(2)
                ffn(L)

        ov = out_d.rearrange("(t p) d -> p t d", p=128)
        for t in range(16):
            kb.dma('sp', ov[:, t, :], h[:, 2 + t, :], r=[('h', 2 + t)], w=[('out', t)])
        kb.wait_all('sp', [('out', t) for t in range(16)])
        print("instructions:", kb.nins, kb.cnt)
    return nc


def _consts():
    t = np.arange(2048)
    inv = 10000.0 ** (-np.arange(16, dtype=np.float32) / 16)
    row = (t // 64).astype(np.float32)[:, None] * inv
    col = (t % 64).astype(np.float32)[:, None] * inv
    C = np.ones((64, TOK), np.float32)
    S = np.zeros((64, TOK), np.float32)
    for d in range(64):
        ang = (row if d < 32 else col)[:, d % 16]
        C[d, 256:] = np.cos(ang)
        S[d, 256:] = np.sin(ang) * (-1.0 if (d % 32) < 16 else 1.0)
    ropeC = np.concatenate([C, C], 0)
    ropeS = np.concatenate([S, S], 0)
    i = np.arange(128)[:, None]
    c = np.arange(512)[None, :]
    wmask = np.stack([(np.abs(dl * 128 + i - c) <= 128).astype(np.float32) for dl in range(-1, 5)], axis=1)
    s_ = np.arange(64)[:, None]
    t_ = np.arange(64)[None, :]
    mall = np.zeros((128, 2, 512), np.float32)
    for d in range(2):
        strict = (s_ < t_) if d == 0 else (s_ > t_)
        incl = (s_ <= t_) if d == 0 else (s_ >= t_)
        for hh in range(2):
            rs = slice(hh * 64, hh * 64 + 64)
            for off in (0, 192):
                mall[rs, d, off + hh * 64: off + hh * 64 + 64] = strict
                mall[rs, d, off + 128: off + 192] = incl
            mall[rs, d, 384 + hh * 64: 384 + hh * 64 + 64] = strict.T
    ii = np.arange(64)
    lmk = np.zeros((128, 2, 7, 128), np.float32)
    for lv in range(6):
        b = 1 << lv
        LMv = ((ii[:, None] // (2 * b) == ii[None, :] // (2 * b)) & ((ii[:, None] // b) % 2 == 1) & ((ii[None, :] // b) % 2 == 0))
        for hh in range(2):
            rs = slice(hh * 64, hh * 64 + 64)
            if lv == 0:
                lmk[rs, 0, 0, rs] = LMv.T
                lmk[rs, 1, 0, rs] = LMv
                lmk[rs, 0, 6, rs] = LMv
                lmk[rs, 1, 6, rs] = LMv.T
            else:
                lmk[rs, 0, lv, rs] = LMv
                lmk[rs, 1, lv, rs] = LMv.T
    mscan = np.ones((128, 512), np.float32)
    mscan[:, ::64] = 0
    blkf = np.zeros((128, 128), np.float32)
    blkf[:64, :64] = 1
    blkf[64:, 64:] = 1
    return ropeC, ropeS, wmask, mall, mscan, blkf, lmk


def _rwp(inp):
    out = np.zeros((2, 128, 66), np.float32)
    for j in range(2):
        mu = np.stack([inp["a_mu_prev"][j], inp["a_mu_next"][j]], -1)
        out[j, :, 0:30] = mu.reshape(15, 128, 2).transpose(1, 0, 2).reshape(128, 30)
        pl = lambda v: v.reshape(4, 128).T
        out[j, :, 30:38] = np.stack([pl(inp["a_w0"][j, d]) for d in range(2)], -1).reshape(128, 8)
        out[j, :, 38:46] = np.stack([pl(inp["a_a0"][j, d]) for d in range(2)], -1).reshape(128, 8)
        out[j, :, 46:50] = pl(inp["a_k_k"][j])
        out[j, :, 50:54] = pl(inp["a_k_a"][j])
        out[j, :, 54:58] = pl(inp["a_r_k"][j].reshape(512))
        out[j, :, 58:62] = pl(inp["a_gn_w"][j])
        out[j, :, 62:66] = pl(inp["a_gn_b"][j])
    return out


def _c_allowed(tq, tk):
    r, c = tq // 64, tq % 64
    kr, kc = tk // 64, tk % 64
    r0 = np.clip(r - 4, 0, 24)
    c0 = np.clip(c - 8, 0, 48)
    ok = (kr >= r0) & (kr < r0 + 8) & (kc >= c0) & (kc < c0 + 16)
    return ok, kr - r + 7, np.clip(kc - c + 15, 0, 30)


def c_class(m):
    return {0: 0, 1: 1, 14: 3, 15: 4}.get(m, 2)


def _c_tiles():
    tiles, n = {}, 0
    for cls, m in enumerate((0, 1, 7, 14, 15)):
        lst = []
        tq = m * 128 + np.arange(128)[None, :]
        for dl in range(-3, 4):
            if not (0 <= m + dl < 16):
                continue
            tk = (m + dl) * 128 + np.arange(128)[:, None]
            ok, _, _ = _c_allowed(tq, tk)
            if ok.any():
                lst.append((dl, n))
                n += 1
        tiles[cls] = lst
    return tiles, n


C_TILES, NCT = _c_tiles()


def _cbias(rpb):
    out = np.full((2, 8, 128, NCT + 1, 128), -30000.0, np.float32)
    for cls, m in enumerate((0, 1, 7, 14, 15)):
        tq = m * 128 + np.arange(128)[None, :]
        for (dl, ti) in C_TILES[cls]:
            tk = (m + dl) * 128 + np.arange(128)[:, None]
            ok, dr, dc = _c_allowed(tq, tk)
            dr = np.clip(dr, 0, 14)
            g = rpb[:, :, dr, dc]
            out[:, :, :, ti, :] = np.where(ok[None, None], g, np.float32(-30000.0))
    return out.reshape(2, 8, 128, (NCT + 1) * 128)


def _odd_cols():
    cq = np.arange(512)
    ck = 512 + np.arange(512)
    cv = 1024 + np.arange(512)
    q = 1536 + np.arange(512)
    k = 2048 + np.arange(128)
    v = 2176 + np.arange(128)
    pm = _perm64()
    hq = lambda h: q[h * 64:(h + 1) * 64]
    cols = list(cq) + list(ck) + list(cv)
    for c in range(4):
        cols += list(hq(c)) + list(hq(4 + c))
    for c in range(4):
        cols += list(hq(c)[pm]) + list(hq(4 + c)[pm])
    k0, k1 = k[:64], k[64:]
    cols += list(k0) + list(k1)
    cols += list(k0[pm]) + list(k1[pm])
    cols += list(v)
    return np.array(cols)


def _dgain(inp):
    pm = _perm64()
    out = np.zeros((2, 128, 4), np.float32)
    for j in range(2):
        qg, kg = inp["d_q_gain"][j], inp["d_k_gain"][j]
        for i, v in enumerate((qg, qg[pm], kg, kg[pm])):
            out[j, :, i] = np.concatenate([v, v])
    return out


def _perm64():
    p = np.arange(64)
    return np.where((p % 32) < 16, p + 16, p - 16)


def _even_cols():
    A_IN = 1920
    cols = list(range(A_IN))
    q = A_IN + np.arange(512)
    k = A_IN + 512 + np.arange(128)
    v = A_IN + 640 + np.arange(128)
    pm = _perm64()
    hq = lambda h: q[h * 64:(h + 1) * 64]
    for c in range(4):
        cols += list(hq(c)) + list(hq(4 + c))
    for c in range(4):
        cols += list(hq(c)[pm]) + list(hq(4 + c)[pm])
    k0, k1 = k[:64], k[64:]
    cols += list(k0) + list(k1)
    cols += list(k0[pm]) + list(k1[pm])
    cols += list(v)
    return np.array(cols)


def host_inputs(inp, b, consts=None):
    f = lambda a: np.ascontiguousarray(a, dtype=np.float32)
    ropeC, ropeS, wmask, mall, mscan, blkf, lmk = consts if consts is not None else _consts()
    cc = np.stack([inp["c"][b], inp["c_ctx"]], axis=-1)
    cvec = cc.reshape(8, 128, 2).transpose(1, 0, 2)
    gains = np.stack([inp["g_pre_mix"], inp["g_post_mix"], inp["g_pre_ff"], inp["g_post_ff"]], axis=1)
    sel = np.zeros((2, 2, 128), np.float32)
    sel[0, 0] = 1
    sel[1, 1] = 1
    return {
        "x": f(inp["x"][b]), "ctx": f(inp["ctx"][b]), "cvec": f(cvec),
        "w_ada": f(inp["w_ada"]), "b_ada": f(inp["b_ada"]),
        "gains2": f(np.stack([gains] * 2, axis=1)),
        "w_ff1": f(inp["w_ff1"]), "w_ff2": f(inp["w_ff2"]),
        "identf_in": np.eye(128, dtype=np.float32), "sel_in": sel,
        "w_out": f(inp["w_out"]), "w_even_x": f(inp["w_in_even"][:, :, _even_cols()]),
        "rwp_in": _rwp(inp), "a_w2s": f(inp["a_w2"].reshape(2, 128, 512)), "a_a2s": f(inp["a_a2"].reshape(2, 128, 512)),
        "a_g2": f(inp["a_g2"]), "mall_in": f(mall), "mscan_in": f(mscan), "lmk_in": f(lmk), "blkf_in": f(blkf),
        "w_odd_x": f(inp["w_in_odd"][:, :, _odd_cols()]), "cbias_in": _cbias(inp["c_rpb"]), "dgain_in": _dgain(inp),
        "wmask_in": f(wmask), "b_sink": f(inp["b_sink"]), "ropeC_in": f(ropeC), "ropeS_in": f(ropeS),
    }


def kernel(**inputs):
    inp = {k: np.asarray(v) for k, v in inputs.items()}
    nc = build()
    in_maps = [host_inputs(inp, b) for b in range(8)]
    res = run_bass_kernel_spmd(nc, in_maps, core_ids=list(range(8)))
    return np.stack([r["out"] for r in res.results], axis=0).astype(np.float32)
```

```python
import contextlib
import numpy as np
import concourse.bass as bass
import concourse.mybir as mybir
from concourse.bass_utils import run_bass_kernel_spmd

F32 = mybir.dt.float32
BF16 = mybir.dt.bfloat16
AF = mybir.ActivationFunctionType
ALU = mybir.AluOpType
AX = mybir.AxisListType

D = 1024
NT = 18
TOK = NT * 128
DEPTH = 4
DFF = 4096
EPS = 1e-6


class KB:
    def __init__(s, nc, es):
        s.nc = nc
        s.es = es
        s.E = {'pe': nc.tensor, 'act': nc.scalar, 'dve': nc.vector, 'pool': nc.gpsimd, 'sp': nc.sync}
        s.sem = {}
        s.cnt = {e: 0 for e in s.E}
        s.known = {e: {} for e in s.E}
        s.lastw = {}
        s.readers = {}
        s.ndma = 24
        s.dma_uses = [0] * s.ndma
        s.dma_next = 0
        s.same_sync = {'pe': False, 'act': False, 'dve': True, 'pool': False, 'sp': False}
        s.nins = 0

    EP = 16384

    def _sem(s, key):
        if key not in s.sem:
            nm = "_".join(str(k) for k in key)
            s.sem[key] = s.es.enter_context(s.nc.semaphore("s_" + nm))
        return s.sem[key]

    def _need(s, e, F, c):
        if F[0] == 'e' and F[1] == e and not s.same_sync[e]:
            return False
        k = s.known[e]
        if F[0] == 'e':
            if k.get(('ep', F[1]), -1) > F[2]:
                return False
        if k.get(F, 0) >= c:
            return False
        k[F] = c
        if F[0] == 'e':
            k[('ep', F[1])] = max(k.get(('ep', F[1]), -1), F[2])
        return True

    def _wait(s, e, F, c):
        if s._need(e, F, c):
            s.E[e].wait_ge(s._sem(F), c)
            s.nins += 1

    def _deps(s, e, r, w, extra=()):
        need = {}
        for (F, c) in extra:
            need[F] = max(need.get(F, 0), c)
        for x in r:
            t = s.lastw.get(x)
            if t is not None:
                need[t[0]] = max(need.get(t[0], 0), t[1])
        for x in w:
            t = s.lastw.get(x)
            if t is not None:
                need[t[0]] = max(need.get(t[0], 0), t[1])
            for F, c in s.readers.get(x, {}).items():
                need[F] = max(need.get(F, 0), c)
        lst = [(F, c) for F, c in need.items() if s._need(e, F, c)]
        for F, c in lst[:-1]:
            s.E[e].wait_ge(s._sem(F), c)
            s.nins += 1
        return lst[-1] if lst else None

    def _upd(s, tok, r, w):
        for x in r:
            d = s.readers.setdefault(x, {})
            d[tok[0]] = max(d.get(tok[0], 0), tok[1])
        for x in w:
            s.lastw[x] = tok
            s.readers[x] = {}

    def op(s, e, fn, r=(), w=(), inc=True):
        lw = s._deps(e, r, w)
        ins = fn(s.E[e])
        if lw is not None:
            ins._wait_ge(s._sem(lw[0]), lw[1])
        F = ('e', e, s.cnt[e] // s.EP)
        c = s.cnt[e] % s.EP + 1
        s.nins += 1
        if inc:
            s.cnt[e] += 1
            ins.then_inc(s._sem(F), 1)
        tok = (F, c)
        s._upd(tok, r, w)
        return tok

    def dma(s, q, out, in_, r=(), w=(), **kw):
        i = s.dma_next
        s.dma_next = (i + 1) % s.ndma
        F = ('d', i)
        extra = [(F, 16 * s.dma_uses[i])] if s.dma_uses[i] > 0 else []
        lw = s._deps(q, r, w, extra)
        ins = s.E[q].dma_start(out=out, in_=in_, **kw)
        if lw is not None:
            ins._wait_ge(s._sem(lw[0]), lw[1])
        s.nins += 1
        s.dma_uses[i] += 1
        ins.then_inc(s._sem(F), 16)
        tok = (F, 16 * s.dma_uses[i])
        s._upd(tok, r, w)
        return tok

    def wait_all(s, e, keys):
        need = {}
        for x in keys:
            t = s.lastw.get(x)
            if t is not None:
                need[t[0]] = max(need.get(t[0], 0), t[1])
            for F, c in s.readers.get(x, {}).items():
                need[F] = max(need.get(F, 0), c)
        for F, c in need.items():
            s._wait(e, F, c)

    def barrier(s, keys):
        for e in ('pe', 'dve', 'act', 'pool', 'sp'):
            s.wait_all(e, keys)


def build(nlayers=DEPTH, mix=True, dbg=None, rwkv=True):
    nc = bass.Bass("TRN2", target_bir_lowering=False)
    dt_in = lambda name, shape: nc.dram_tensor(name, list(shape), F32, kind="ExternalInput").ap()
    x_d = dt_in("x", (2048, D))
    ctx_d = dt_in("ctx", (256, D))
    cvec_d = dt_in("cvec", (128, 8, 2))
    wada_d = dt_in("w_ada", (DEPTH, D, 6 * D))
    bada_d = dt_in("b_ada", (DEPTH, 6 * D))
    gains_d = dt_in("gains2", (DEPTH, 2, 4, D))
    wff1_d = dt_in("w_ff1", (DEPTH, D, DFF))
    wff2_d = dt_in("w_ff2", (DEPTH, DFF, D))
    identf_d = dt_in("identf_in", (128, 128))
    sel_d = dt_in("sel_in", (2, 2, 128))
    wout_d = dt_in("w_out", (DEPTH, D, D))
    wev_d = dt_in("w_even_x", (2, D, 26 * 128))
    wmask_d = dt_in("wmask_in", (128, 6, 512))
    sink_d = dt_in("b_sink", (2, 8))
    ropeC_d = dt_in("ropeC_in", (128, TOK))
    ropeS_d = dt_in("ropeS_in", (128, TOK))
    wod_d = dt_in("w_odd_x", (2, D, 23 * 128))
    cbias_d = dt_in("cbias_in", (2, 8, 128, (NCT + 1) * 128))
    dgain_d = dt_in("dgain_in", (2, 128, 4))
    rwp_d = dt_in("rwp_in", (2, 128, 66))
    w2_d = dt_in("a_w2s", (2, 128, 512))
    a2_d = dt_in("a_a2s", (2, 128, 512))
    g2_d = dt_in("a_g2", (2, 128, 512))
    mall_d = dt_in("mall_in", (128, 2, 512))
    mscan_d = dt_in("mscan_in", (128, 512))
    lmk_d = dt_in("lmk_in", (128, 2, 7, 128))
    blkf_d = dt_in("blkf_in", (128, 128))
    pA_d = nc.dram_tensor("pA_scratch", [15, 128, TOK], BF16, kind="Internal").ap()
    w1b_d = nc.dram_tensor("w1b_scratch", [DEPTH, D, DFF], BF16, kind="Internal").ap()
    w2b_d = nc.dram_tensor("w2b_scratch", [DEPTH, DFF, D], BF16, kind="Internal").ap()
    out_d = nc.dram_tensor("out", [2048, D], F32, kind="ExternalOutput").ap()
    if dbg in ('mixT', 'mixT1', 'mixT2', 'mixT3'):
        dbg_d = nc.dram_tensor("dbg", [128, 8, TOK], BF16, kind="ExternalOutput").ap()

    _uid = [0]

    def SBT(name, shape, dt):
        _uid[0] += 1
        return nc.sbuf_tensor(f"{name}_{_uid[0]}", shape, dt)

    with contextlib.ExitStack() as es:
        kb = KB(nc, es)
        sb = lambda name, shape, dt=F32: es.enter_context(SBT(name, list(shape), dt))
        h = sb("h", (128, NT, D))
        identf = sb("identf", (128, 128))
        identb = sb("identb", (128, 128), BF16)
        sel = sb("sel", (2, 2, 128))
        ones_f = sb("ones_f", (128, 128))
        ones_b = sb("ones_b", (128, 128), BF16)
        sT = sb("sT", (128, 8, 2))
        modT = sb("modT", (128, 4, 8, 2))
        G = sb("G", (128, 4, D))
        ss = sb("ss", (128, 4))
        epsT = sb("epsT", (128, 2))
        rstd = sb("rstd", (128, 2))
        U = {'t': None}

        @contextlib.contextmanager
        def uT_scope():
            with SBT("uT", [128, 8, TOK], BF16) as t:
                U['t'] = t
                yield t
                kb.barrier([('uT', i) for i in range(NT)])
            U['t'] = None
        RP = {}
        ps = [es.enter_context(nc.psum_tensor(f"ps{i}", [128, 512], F32)) for i in range(8)]
        psn = [0]

        def nps():
            i = psn[0]
            psn[0] = (i + 1) % 8
            return i

        xv = x_d.rearrange("(t p) d -> p t d", p=128)
        cv = ctx_d.rearrange("(t p) d -> p t d", p=128)
        for t in range(2):
            kb.dma('sp', h[:, t, :], cv[:, t, :], w=[('h', t)])
        for t in range(16):
            kb.dma('sp', h[:, 2 + t, :], xv[:, t, :], w=[('h', 2 + t)])
        kb.dma('sp', identf[:], identf_d, w=['identf'])
        kb.dma('sp', sel[:], sel_d, w=['sel'])
        kb.dma('sp', sT[:], cvec_d, w=['sT'])
        kb.op('dve', lambda e: e.tensor_copy(out=identb[:], in_=identf[:]), r=['identf'], w=['identb'])
        kb.op('dve', lambda e: e.memset(ones_f[:], 1.0), w=['ones_f'])
        kb.op('dve', lambda e: e.memset(epsT[:], EPS), w=['epsT'])
        kb.op('dve', lambda e: e.memset(ones_b[:], 1.0), w=['ones_b'])
        kb.op('act', lambda e: e.activation(out=sT[:], in_=sT[:], func=AF.Silu), r=['sT'], w=['sT'])

        def adaln(L):
            with contextlib.ExitStack() as es2:
                sb2 = lambda name, shape, dt=F32: es2.enter_context(SBT(name, list(shape), dt))
                modrow = sb2("modrow", (2, 6 * D))
                grow = sb2("grow", (2, 4, D))
                wt = [sb2(f"wada{i}", (128, 9, 512)) for i in range(2)]
                kb.dma('sp', grow[:], gains_d[L], w=['grow'])
                wv = wada_d[L].rearrange("(kc p) n -> p kc n", p=128)
                for g in range(12):
                    b = g % 2
                    kb.dma('sp', wt[b][:, 0:8, :], wv[:, :, g * 512:(g + 1) * 512], w=[('wada', b)])
                    kb.dma('sp', wt[b][0:1, 8, :], bada_d[L:L + 1, g * 512:(g + 1) * 512], w=[('wadab', b)])
                    pi = nps()
                    for kc in range(8):
                        kb.op('pe', lambda e, kc=kc, b=b, pi=pi: e.matmul(
                            ps[pi][0:2, :], lhsT=sT[:, kc, :], rhs=wt[b][:, kc, :], start=(kc == 0), stop=False),
                            r=['sT', ('wada', b)], w=[('ps', pi)], inc=False)
                    kb.op('pe', lambda e, b=b, pi=pi: e.matmul(
                        ps[pi][0:2, :], lhsT=ones_f[0:1, 0:2], rhs=wt[b][0:1, 8, :], start=False, stop=True),
                        r=['ones_f', ('wadab', b)], w=[('ps', pi)])
                    kb.op('dve', lambda e, g=g, pi=pi: e.tensor_copy(
                        out=modrow[:, g * 512:(g + 1) * 512], in_=ps[pi][0:2, :]),
                        r=[('ps', pi)], w=[('modrow', g)])
                seg = lambda i: modrow[:, i * D:(i + 1) * D]
                mk = lambda i: [('modrow', 2 * i), ('modrow', 2 * i + 1)]
                kb.op('dve', lambda e: e.scalar_tensor_tensor(out=seg(1), in0=seg(1), scalar=1.0, in1=grow[:, 0, :],
                                                              op0=ALU.add, op1=ALU.mult), r=mk(1) + ['grow'], w=mk(1))
                kb.op('dve', lambda e: e.tensor_tensor(out=seg(2), in0=seg(2), in1=grow[:, 1, :], op=ALU.mult),
                      r=mk(2) + ['grow'], w=mk(2))
                kb.op('dve', lambda e: e.scalar_tensor_tensor(out=seg(4), in0=seg(4), scalar=1.0, in1=grow[:, 2, :],
                                                              op0=ALU.add, op1=ALU.mult), r=mk(4) + ['grow'], w=mk(4))
                kb.op('dve', lambda e: e.tensor_tensor(out=seg(5), in0=seg(5), in1=grow[:, 3, :], op=ALU.mult),
                      r=mk(5) + ['grow'], w=mk(5))
                pi = nps()
                for vi, sg in enumerate((1, 0, 4, 3)):
                    for kc in range(8):
                        kb.op('pe', lambda e, vi=vi, sg=sg, kc=kc, pi=pi: e.transpose(
                            ps[pi][:, (vi * 8 + kc) * 2:(vi * 8 + kc) * 2 + 2],
                            modrow[:, sg * D + kc * 128: sg * D + (kc + 1) * 128], identf[0:2, 0:2]),
                            r=mk(sg) + ['identf'], w=[('ps', pi)])
                kb.op('dve', lambda e, pi=pi: e.tensor_copy(out=modT[:].rearrange("p a b c -> p (a b c)"),
                                                           in_=ps[pi][:, 0:64]), r=[('ps', pi)], w=['modT'])
                for gi, sg in enumerate((2, 5)):
                    for st in range(2):
                        for hh in range(2):
                            pi = nps()
                            kb.op('pe', lambda e, sg=sg, st=st, hh=hh, pi=pi: e.matmul(
                                ps[pi][:, :], lhsT=sel[:, st, :], rhs=modrow[:, sg * D + hh * 512: sg * D + (hh + 1) * 512],
                                start=True, stop=True), r=mk(sg) + ['sel'], w=[('ps', pi)])
                            kb.op('act', lambda e, gi=gi, st=st, hh=hh, pi=pi: e.copy(
                                out=G[:, gi * 2 + st, hh * 512:(hh + 1) * 512], in_=ps[pi][:, :]),
                                r=[('ps', pi)], w=[('G', gi * 2 + st)])
                kb.barrier(['grow', ('wada', 0), ('wada', 1), ('wadab', 0), ('wadab', 1)] + [('modrow', i) for i in range(12)])

        def prep(vi):
            with contextlib.ExitStack() as esx:
                xn = [esx.enter_context(SBT(f"xn{i}", [128, D], BF16)) for i in range(2)]
                prep_(vi, xn)
                kb.barrier([('xn', 0), ('xn', 1)])

        def prep_(vi, xn):
            def stage_a(t):
                b = t % 2
                kb.op('act', lambda e: e.activation(out=xn[b][:], in_=h[:, t, :], func=AF.Square,
                                                    accum_out=ss[:, b:b + 1]),
                      r=[('h', t)], w=[('xn', b), ('ss', b)])
                kb.op('act', lambda e: e.activation(out=rstd[:, b:b + 1], in_=ss[:, b:b + 1], func=AF.Sqrt,
                                                    bias=epsT[:, 0:1], scale=1.0 / D),
                      r=[('ss', b), 'epsT'], w=[('rstd', b)])
                kb.op('dve', lambda e: e.reciprocal(out=rstd[:, b:b + 1], in_=rstd[:, b:b + 1]),
                      r=[('rstd', b)], w=[('rstd', b)])
                kb.op('act', lambda e: e.activation(out=xn[b][:], in_=h[:, t, :], func=AF.Copy,
                                                    scale=rstd[:, b:b + 1]),
                      r=[('h', t), ('rstd', b)], w=[('xn', b)])

            def stage_b(t):
                st = 1 if t < 2 else 0
                b = t % 2
                pi = nps()
                pb = ps[pi][:].bitcast(BF16)
                for kc in range(8):
                    kb.op('pe', lambda e, kc=kc: e.transpose(pb[:, kc * 128:(kc + 1) * 128],
                                                             xn[b][:, kc * 128:(kc + 1) * 128], identb[:]),
                          r=[('xn', b), 'identb'], w=[('ps', pi)], inc=(kc == 7))
                for kc in range(8):
                    dstk = U['t'][:, kc, t * 128:(t + 1) * 128]
                    Ak = modT[:, vi, kc, st:st + 1]
                    Sk = modT[:, vi + 1, kc, st:st + 1]
                    if kc % 2 == 0:
                        kb.op('act', lambda e, kc=kc, dstk=dstk, Ak=Ak, Sk=Sk: e.activation(
                            out=dstk, in_=pb[:, kc * 128:(kc + 1) * 128], func=AF.Identity, bias=Sk, scale=Ak),
                            r=[('ps', pi), 'modT'], w=[('uT', t)])
                    else:
                        kb.op('dve', lambda e, kc=kc, dstk=dstk, Ak=Ak, Sk=Sk: e.tensor_scalar(
                            out=dstk, in0=pb[:, kc * 128:(kc + 1) * 128], scalar1=Ak, scalar2=Sk, op0=ALU.mult, op1=ALU.add),
                            r=[('ps', pi), 'modT'], w=[('uT', t)])
            stage_a(0)
            for t in range(NT):
                if t + 1 < NT:
                    stage_a(t + 1)
                stage_b(t)

        def resid(t, pis, gi, ftmp):
            st = 1 if t < 2 else 0
            for hh in range(2):
                kb.op('act', lambda e, hh=hh: e.activation(out=ftmp[:, hh * 512:(hh + 1) * 512], in_=ps[pis[hh]][:, :],
                                                          func=AF.Square, accum_out=ss[:, 2 + hh:3 + hh]),
                      r=[('ps', pis[hh])], w=[('ss', 2 + hh), 'ftmp'])
            kb.op('dve', lambda e: e.tensor_tensor(out=ss[:, 2:3], in0=ss[:, 2:3], in1=ss[:, 3:4], op=ALU.add),
                  r=[('ss', 2), ('ss', 3)], w=[('ss', 2)])
            kb.op('act', lambda e: e.activation(out=ss[:, 2:3], in_=ss[:, 2:3], func=AF.Sqrt, bias=epsT[:, 0:1],
                                                scale=1.0 / D), r=[('ss', 2), 'epsT'], w=[('ss', 2)])
            kb.op('dve', lambda e: e.reciprocal(out=ss[:, 2:3], in_=ss[:, 2:3]), r=[('ss', 2)], w=[('ss', 2)])
            for hh in range(2):
                sl = slice(hh * 512, (hh + 1) * 512)
                kb.op('dve', lambda e, hh=hh, sl=sl: e.scalar_tensor_tensor(
                    out=ftmp[:, sl], in0=ps[pis[hh]][:, :], scalar=ss[:, 2:3], in1=G[:, gi * 2 + st, sl],
                    op0=ALU.mult, op1=ALU.mult), r=[('ps', pis[hh]), ('ss', 2), ('G', gi * 2 + st)], w=['ftmp'])
            kb.op('pool', lambda e: e.tensor_tensor(out=h[:, t, :], in0=h[:, t, :], in1=ftmp[:], op=ALU.add),
                  r=['ftmp', ('h', t)], w=[('h', t)])

        def ffn_convert(L):
            for i in range(8):
                kb.dma('pool', w1b_d[L, i * 128:(i + 1) * 128, :], wff1_d[L, i * 128:(i + 1) * 128, :], w=[('w1b', L)])
            for i in range(8):
                kb.dma('pool', w2b_d[L, i * 512:(i + 1) * 512, :], wff2_d[L, i * 512:(i + 1) * 512, :], w=[('w2b', L)])

        def ffn(L):
            with contextlib.ExitStack() as es2:
                sb2 = lambda name, shape, dt=F32: es2.enter_context(SBT(name, list(shape), dt))
                h1T = sb2("h1T", (128, 32, 512), BF16)
                w1 = [sb2(f"w1_{i}", (128, 8, 512), BF16) for i in range(2)]
                w2 = [sb2(f"w2_{i}", (128, 4, D), BF16) for i in range(2)]
                rl = [sb2(f"rl{i}", (128, 512)) for i in range(2)]
                ftmp = sb2("ftmp", (128, D))
                w1v = w1b_d[L].rearrange("(kc p) n -> p kc n", p=128)
                w2v = w2b_d[L].rearrange("(fc p) n -> p fc n", p=128)
                groups = [(0, 512), (512, 512), (1024, 512), (1536, 512), (2048, 256)]
                if L == DEPTH - 1:
                    groups = [(256, 512), (768, 512), (1280, 512), (1792, 512)]
                nld = [0, 0]
                for (t0, tn) in groups:
                    for fg in range(8):
                        b = nld[0] % 2
                        nld[0] += 1
                        kb.dma('sp', w1[b][:], w1v[:, :, fg * 512:(fg + 1) * 512], r=[('w1b', L)], w=[('w1', b)])
                        for f4 in range(4):
                            fc = fg * 4 + f4
                            pi = nps()
                            for kc in range(8):
                                kb.op('pe', lambda e, kc=kc, b=b, f4=f4, pi=pi: e.matmul(
                                    ps[pi][:, 0:tn], lhsT=w1[b][:, kc, f4 * 128:(f4 + 1) * 128], rhs=U['t'][:, kc, t0:t0 + tn],
                                    start=(kc == 0), stop=(kc == 7)),
                                    r=[('w1', b)] + [('uT', t) for t in range(t0 // 128, (t0 + tn) // 128)], w=[('ps', pi)],
                                    inc=(kc == 7))
                            rb = fc % 2
                            kb.op('act', lambda e, rb=rb, pi=pi: e.activation(out=rl[rb][:, 0:tn], in_=ps[pi][:, 0:tn],
                                                                             func=AF.Relu), r=[('ps', pi)], w=[('rl', rb)])
                            kb.op('pool', lambda e, rb=rb, fc=fc: e.tensor_tensor(
                                out=h1T[:, fc, 0:tn], in0=rl[rb][:, 0:tn], in1=rl[rb][:, 0:tn], op=ALU.mult),
                                r=[('rl', rb)], w=[('h1T', fc)])
                    ntile = tn // 128
                    for fg in range(8):
                        b = nld[1] % 2
                        nld[1] += 1
                        kb.dma('sp', w2[b][:], w2v[:, fg * 4:(fg + 1) * 4, :], r=[('w2b', L)], w=[('w2', b)])
                        for f4 in range(4):
                            fc = fg * 4 + f4
                            for tt in range(ntile):
                                for hh in range(2):
                                    pi = tt * 2 + hh
                                    kb.op('pe', lambda e, fc=fc, f4=f4, tt=tt, hh=hh, pi=pi, b=b: e.matmul(
                                        ps[pi][:, :], lhsT=h1T[:, fc, tt * 128:(tt + 1) * 128],
                                        rhs=w2[b][:, f4, hh * 512:(hh + 1) * 512], start=(fc == 0), stop=(fc == 31)),
                                        r=[('h1T', fc), ('w2', b)], w=[('ps', pi)],
                                        inc=(fc == 31 or (tt == ntile - 1 and hh == 1)))
                    for tt in range(ntile):
                        resid(t0 // 128 + tt, (tt * 2, tt * 2 + 1), 1, ftmp)
                kb.barrier([('w1', 0), ('w1', 1), ('w2', 0), ('w2', 1), ('rl', 0), ('rl', 1), 'ftmp']
                           + [('h1T', i) for i in range(32)])


        GROUPS = [(0, 256), (256, 512), (768, 512), (1280, 512), (1792, 512)]

        def tiles_of(t0, tn):
            return list(range(t0 // 128, (t0 + tn) // 128))

        def proj_fm(wsrc, nchunks, wb, evac, groups=GROUPS):
            wv = wsrc.rearrange("(kc p) n -> p kc n", p=128)
            for cg in range(0, nchunks, 1):
                n = 1
                b = wb['n'] % len(wb['bufs'])
                wb['n'] += 1
                buf = wb['bufs'][b]
                kb.dma('pool', buf[:, :, 0:n * 128], wv[:, :, cg * 128:(cg + n) * 128], w=[('wb', b)])
                for c in range(n):
                    for (t0, tn) in groups:
                        pi = nps()
                        for kc in range(8):
                            kb.op('pe', lambda e, kc=kc, c=c, pi=pi, t0=t0, tn=tn, buf=buf: e.matmul(
                                ps[pi][:, 0:tn], lhsT=buf[:, kc, c * 128:(c + 1) * 128], rhs=U['t'][:, kc, t0:t0 + tn],
                                start=(kc == 0), stop=(kc == 7)),
                                r=[('wb', b)] + [('uT', t) for t in tiles_of(t0, tn)], w=[('ps', pi)], inc=(kc == 7))
                        evac(cg + c, t0, tn, pi)

        def proj_tm(wsrc, ncols, wb, evac):
            wv = wsrc.rearrange("(kc p) n -> p kc n", p=128)
            b = wb['n'] % len(wb['bufs'])
            wb['n'] += 1
            buf = wb['bufs'][b]
            kb.dma('pool', buf[:, :, 0:ncols], wv, w=[('wb', b)])
            for t in range(NT):
                pi = nps()
                for kc in range(8):
                    kb.op('pe', lambda e, kc=kc, pi=pi, t=t: e.matmul(
                        ps[pi][:, 0:ncols], lhsT=U['t'][:, kc, t * 128:(t + 1) * 128], rhs=buf[:, kc, 0:ncols],
                        start=(kc == 0), stop=(kc == 7)), r=[('wb', b), ('uT', t)], w=[('ps', pi)], inc=(kc == 7))
                evac(t, pi)

        def attn_group(A, qT, qkey, base, kT, kkey, vt2, vkey, q0, nq, keys, dst, dkey, seed=None):
            rows = slice(base, base + 64)
            g = A['g'] % 4
            A['g'] += 1
            pnum = g
            drows = slice(64 - base, 128 - base)
            first = True
            nk = len(keys)
            LA = 3
            sbank = {}

            def crange(ki):
                k = keys[ki]
                return k[2] if len(k) > 2 else (0, nq)

            def issue_S(ki):
                kt = keys[ki][0]
                c0, c1 = crange(ki)
                pi = 4 + (A['s'] % 4)
                A['s'] += 1
                sbank[ki] = pi
                kb.op('pe', lambda e: e.matmul(
                    ps[pi][:, c0:c1], lhsT=kT[rows, kt * 128:(kt + 1) * 128], rhs=qT[rows, q0 + c0:q0 + c1],
                    start=True, stop=True), r=[kkey, qkey], w=[('ps', pi)])
            for ki in range(min(LA, nk)):
                issue_S(ki)
            for ki in range(nk):
                kt, mask = keys[ki][0], keys[ki][1]
                c0, c1 = crange(ki)
                last = ki == nk - 1
                if ki + LA < nk:
                    issue_S(ki + LA)
                pi = sbank[ki]
                pb = A['p'] % len(A['PT'])
                A['p'] += 1
                PT = A['PT'][pb]
                kb.op('act', lambda e, pi=pi, PT=PT: e.activation(out=PT[:, c0:c1], in_=ps[pi][:, c0:c1], func=AF.Exp,
                                                                 scale=0.125), r=[('ps', pi)], w=[('PT', pb)])
                if mask is not None:
                    mk_ap, mk_key = mask
                    pieces = mk_ap if isinstance(mk_ap, list) else [(c0, c1, mk_ap)]
                    for (p0, p1, pap) in pieces:
                        kb.op('dve', lambda e, PT=PT, pap=pap, p0=p0, p1=p1: e.tensor_tensor(out=PT[:, p0:p1], in0=PT[:, p0:p1],
                                                                                          in1=pap, op=ALU.mult),
                              r=[('PT', pb), mk_key], w=[('PT', pb)])
                kb.op('pe', lambda e, kt=kt, PT=PT, first=first, last=last: e.matmul(
                    ps[pnum][:, c0:c1], lhsT=vt2[:, kt, base:base + 128], rhs=PT[:, c0:c1], start=first, stop=last),
                    r=[vkey, ('PT', pb)], w=[('ps', pnum)])
                first = False
            rd = A['rden']
            if seed is not None:
                kb.op('dve', lambda e: e.tensor_scalar(out=rd[drows, 0:nq], in0=ps[pnum][drows, 0:nq], scalar1=seed[drows, :],
                                                       scalar2=None, op0=ALU.add), r=[('ps', pnum), 'eskb'], w=['rden'])
                kb.op('dve', lambda e: e.reciprocal(out=rd[drows, 0:nq], in_=rd[drows, 0:nq]), r=['rden'], w=['rden'])
            else:
                kb.op('dve', lambda e: e.reciprocal(out=rd[drows, 0:nq], in_=ps[pnum][drows, 0:nq]),
                      r=[('ps', pnum)], w=['rden'])
            kb.op('dve', lambda e: e.tensor_tensor(out=dst, in0=ps[pnum][rows, 0:nq], in1=rd[drows, 0:nq], op=ALU.mult),
                  r=[('ps', pnum), 'rden'], w=[dkey])

        def rope(x, xp, xkey, xpkey):
            kb.op('dve', lambda e: e.tensor_tensor(out=x[:], in0=x[:], in1=RP['C'][:], op=ALU.mult), r=[xkey, 'rope'], w=[xkey])
            kb.op('pool', lambda e: e.tensor_tensor(out=xp[:], in0=xp[:], in1=RP['S'][:], op=ALU.mult), r=[xpkey, 'rope'], w=[xpkey])
            kb.op('dve', lambda e: e.tensor_tensor(out=x[:], in0=x[:], in1=xp[:], op=ALU.add), r=[xkey, xpkey], w=[xkey])

        def outproj(L, mixT):
            with contextlib.ExitStack() as es2:
                sb2 = lambda name, shape, dt=F32: es2.enter_context(SBT(name, list(shape), dt))
                wo = sb2("wo", (128, 8, D), BF16)
                ftmp = sb2("ftmp_o", (128, D))
                kb.dma('pool', wo[:, 0:4, :], wout_d[L, 0:512, :].rearrange("(kc p) n -> p kc n", p=128), w=['wo'])
                wo2 = wout_d[L, 512:1024, :].rearrange("(hh c p) n -> hh p c n", hh=2, p=64)
                for hh in range(2):
                    kb.dma('pool', wo[hh * 64:(hh + 1) * 64, 4:8, :], wo2[hh], w=['wo'])
                for t in range(2 if L == DEPTH - 1 else 0, NT):
                    pis = (nps(), nps())
                    for hh in range(2):
                        for c in range(8):
                            kb.op('pe', lambda e, c=c, hh=hh, t=t: e.matmul(
                                ps[pis[hh]][:, :], lhsT=mixT[:, c, t * 128:(t + 1) * 128], rhs=wo[:, c, hh * 512:(hh + 1) * 512],
                                start=(c == 0), stop=(c == 7)), r=['wo', ('mixT', c)], w=[('ps', pis[hh])], inc=(c == 7))
                    resid(t, pis, 0, ftmp)
                kb.barrier(['wo', 'ftmp'])


        def rwkv_project(L, es2):
            j = L // 2
            sb2 = lambda name, shape, dt=F32: es2.enter_context(SBT(name, list(shape), dt))
            wb = {'n': 0, 'bufs': [sb2(f"wbr{i}", (128, 8, 128), BF16) for i in range(2)]}
            praw = sb2("praw", (128, TOK))
            psh = sb2("psh", (128, TOK))
            pbf = [sb2(f"pbf{i}", (128, TOK), BF16) for i in range(2)]
            rwp = sb2("rwp_p", (128, 66))
            c0 = sb2("c0", (128, 15))
            kb.dma('sp', rwp[:], rwp_d[j], w=['rwp'])
            mu2 = rwp[:, 0:30].rearrange("p (c two) -> p c two", two=2)
            kb.op('dve', lambda e: e.tensor_tensor(out=c0[:], in0=mu2[:, :, 0], in1=mu2[:, :, 1], op=ALU.add), r=['rwp'], w=['c0'])
            kb.op('dve', lambda e: e.tensor_scalar(out=c0[:], in0=c0[:], scalar1=-1.0, scalar2=1.0, op0=ALU.mult, op1=ALU.add),
                  r=['c0'], w=['c0'])
            for c in range(15):
                def ev(cc, t0, tn, pi):
                    kb.op('act', lambda e: e.copy(out=praw[:, t0:t0 + tn], in_=ps[pi][:, 0:tn]), r=[('ps', pi)], w=['praw'])
                proj_fm(wev_d[j][:, c * 128:(c + 1) * 128], 1, wb, ev)
                mp = rwp[:, 2 * c:2 * c + 1]
                mn = rwp[:, 2 * c + 1:2 * c + 2]
                kb.op('dve', lambda e: e.tensor_scalar(out=psh[:], in0=praw[:], scalar1=c0[:, c:c + 1], scalar2=None, op0=ALU.mult),
                      r=['praw', 'c0'], w=['psh'])
                for (lo, hi) in ((0, 256), (256, TOK)):
                    kb.op('dve', lambda e: e.scalar_tensor_tensor(out=psh[:, lo + 1:hi], in0=praw[:, lo:hi - 1], scalar=mp,
                                                                  in1=psh[:, lo + 1:hi], op0=ALU.mult, op1=ALU.add),
                          r=['praw', 'rwp', 'psh'], w=['psh'])
                    kb.op('dve', lambda e: e.scalar_tensor_tensor(out=psh[:, lo:hi - 1], in0=praw[:, lo + 1:hi], scalar=mn,
                                                                  in1=psh[:, lo:hi - 1], op0=ALU.mult, op1=ALU.add),
                          r=['praw', 'rwp', 'psh'], w=['psh'])
                b = c % 2
                fn = {12: AF.Tanh, 14: AF.Sigmoid}.get(c, AF.Copy)
                kb.op('act', lambda e: e.activation(out=pbf[b][:], in_=psh[:], func=fn), r=['psh'], w=[('pbf', b)])
                kb.dma('sp', pA_d[c], pbf[b][:], r=[('pbf', b)], w=[('pA', c)])
            kb.barrier([('wb', 0), ('wb', 1), 'praw', 'psh', ('pbf', 0), ('pbf', 1), 'rwp', 'c0'])

        def rwkv_passes(L, mixT, es2):
            j = L // 2
            sb2 = lambda name, shape, dt=F32: es2.enter_context(SBT(name, list(shape), dt))
            keys_all = []

            def al(name, shape, dt=F32):
                keys_all.append(name)
                return sb2(name, shape, dt)
            rwp = al("rwp", (128, 66))
            w2s = al("w2s", (128, 128), BF16)
            a2s = al("a2s", (128, 128), BF16)
            g2s = al("g2s", (128, 128), BF16)
            mall = al("mall", (128, 512), BF16)
            mscan = al("mscan", (128, 256))
            blkf = al("blkf", (128, 128))
            blkRK = al("blkRK", (128, 128), BF16)
            gne = al("gne", (128, 1))
            kb.dma('sp', rwp[:], rwp_d[j], w=['rwp'])
            kb.dma('sp', mscan[:], mscan_d[:, 0:256], w=['mscan'])
            kb.dma('sp', blkf[:], blkf_d, w=['blkf'])
            kb.op('dve', lambda e: e.memset(gne[:], 64e-5), w=['gne'])
            w0T = rwp[:, 30:38].rearrange("p (j d) -> p j d", d=2)
            a0T = rwp[:, 38:46].rearrange("p (j d) -> p j d", d=2)
            ksum = al("ksum", (128, TOK), BF16)
            yT = al("yT", (128, TOK))
            gin = al("gin", (128, 5, 256), BF16)
            F = {n: al("f_" + n, (128, 256)) for n in ("lw", "P", "Q", "E", "b", "kd", "kk")}
            F["icl"] = F["Q"]
            BDs = [{n: al(f"bd_{n}{s_}", (128, 4, 128), BF16) for n in ("A", "B", "K", "BE", "KE", "V")} for s_ in range(2)]
            Rts = [al(f"Rt{s_}", (128, 256), BF16) for s_ in range(2)]
            gCs = [al(f"gC{s_}", (128, 4)) for s_ in range(2)]
            G256 = [(0, 256)] + [(256 + 256 * i, 256) for i in range(8)]
            NS = 8
            Ms = [al(f"Ms{i}", (128, 512), BF16) for i in range(NS)]
            DD = [[al(f"DD{i}{k}", (128, 256), BF16) for k in range(2)] for i in range(NS)]
            EE = [al(f"EE{i}", (128, 128), BF16) for i in range(NS)]
            lmk = al("lmk", (128, 7, 128), BF16)
            print("rwkv sbuf remaining", nc.sbuf_bytes_remaining)
            Tk = [al(f"Tk{i}", (128, 384), BF16) for i in range(NS)]
            Vx = [al(f"Vx{i}", (128, 64), BF16) for i in range(NS)]
            Wsb = [al(f"Wsb{i}", (128, 64), BF16) for i in range(2)]
            Up = [al(f"Up{i}", (128, 64), BF16) for i in range(2)]
            Ubd = [al(f"Ubd{i}", (128, 128), BF16) for i in range(2)]
            Hp = [al(f"Hp{i}", (128, 64), BF16) for i in range(2)]
            Hbd = [al(f"Hbd{i}", (128, 128), BF16) for i in range(2)]
            Hm = al("Hm", (128, 64))
            for s_ in range(2):
                for n in BDs[s_]:
                    kb.op('pool', lambda e, n=n, s_=s_: e.memset(BDs[s_][n][:], 0.0), w=[f"bd_{n}{s_}"])
            for i in range(2):
                kb.op('pool', lambda e, i=i: e.memset(Ubd[i][:], 0.0), w=[f"Ubd{i}"])
            half = (slice(0, 64), slice(64, 128))

            def v3(ap, nck):
                return ap.rearrange("p (c t) -> p c t", t=64)

            def gen(pj, d, t0, tn, gs):
                BD, Rt, gC = BDs[gs], Rts[gs], gCs[gs]
                nck = tn // 64
                for i, cidx in enumerate((pj, 4 + pj, 8 + pj, 12, 13)):
                    kb.dma('sp', gin[:, i, 0:tn], pA_d[cidx][:, t0:t0 + tn], r=[('pA', cidx)], w=[('gin', i)])
                rg, kg, vg, wlg, alg = (gin[:, i, 0:tn] for i in range(5))
                dr = slice(d * 64, (d + 1) * 64)
                pc = slice(0, 128)
                f = {n: F[n][:, 0:tn] for n in F}
                pi = nps()
                kb.op('pe', lambda e: e.matmul(ps[pi][:, 0:tn], lhsT=w2s[dr, pc], rhs=gin[dr, 3, 0:tn], start=True, stop=True),
                      r=['w2s', ('gin', 3)], w=[('ps', pi)])
                yield
                kb.op('act', lambda e: e.activation(out=f['lw'], in_=ps[pi][:, 0:tn], func=AF.Sigmoid, bias=w0T[:, pj, d:d + 1]),
                      r=[('ps', pi), 'rwp'], w=['f_lw'])
                yield
                kb.op('dve', lambda e: e.tensor_scalar(out=f['lw'], in0=f['lw'], scalar1=-0.6065306597126334, scalar2=None,
                                                       op0=ALU.mult), r=['f_lw'], w=['f_lw'])
                yield
                pi2 = nps()
                kb.op('pe', lambda e: e.matmul(ps[pi2][:, 0:tn], lhsT=a2s[dr, pc], rhs=gin[dr, 4, 0:tn], start=True, stop=True),
                      r=['a2s', ('gin', 4)], w=[('ps', pi2)])
                yield
                kb.op('act', lambda e: e.activation(out=f['icl'], in_=ps[pi2][:, 0:tn], func=AF.Sigmoid, bias=a0T[:, pj, d:d + 1]),
                      r=[('ps', pi2), 'rwp'], w=['f_Q'])
                yield
                kb.op('dve', lambda e: e.tensor_scalar(out=f['kk'], in0=kg, scalar1=rwp[:, 46 + pj:47 + pj], scalar2=None,
                                                       op0=ALU.mult), r=[('gin', 1), 'rwp'], w=['f_kk'])
                yield
                kb.op('pool', lambda e: e.tensor_tensor(out=f['E'], in0=f['kk'], in1=f['kk'], op=ALU.mult), r=['f_kk'], w=['f_E'])
                yield
                pi3 = nps()
                kb.op('pe', lambda e: e.matmul(ps[pi3][:, 0:tn], lhsT=blkf[:, :], rhs=f['E'], start=True, stop=True),
                      r=['blkf', 'f_E'], w=[('ps', pi3)])
                yield
                kb.op('act', lambda e: e.activation(out=f['E'], in_=ps[pi3][:, 0:tn], func=AF.Sqrt), r=[('ps', pi3)], w=['f_E'])
                yield
                kb.op('dve', lambda e: e.tensor_scalar(out=f['E'], in0=f['E'], scalar1=1e-12, scalar2=None, op0=ALU.max),
                      r=['f_E'], w=['f_E'])
                yield
                kb.op('dve', lambda e: e.reciprocal(out=f['E'], in_=f['E']), r=['f_E'], w=['f_E'])
                yield
                kb.op('dve', lambda e: e.tensor_tensor(out=f['kk'], in0=f['kk'], in1=f['E'], op=ALU.mult), r=['f_kk', 'f_E'], w=['f_kk'])
                yield
                kb.op('dve', lambda e: e.tensor_tensor(out=f['b'], in0=f['kk'], in1=f['icl'], op=ALU.mult),
                      r=['f_kk', 'f_Q'], w=['f_b'])
                yield
                kb.op('dve', lambda e: e.tensor_scalar(out=f['kd'], in0=f['icl'], scalar1=-1.0, scalar2=rwp[:, 50 + pj:51 + pj],
                                                       op0=ALU.add, op1=ALU.mult), r=['f_Q', 'rwp'], w=['f_kd'])
                yield
                kb.op('dve', lambda e: e.scalar_tensor_tensor(out=f['kd'], in0=f['kd'], scalar=1.0, in1=kg, op0=ALU.add, op1=ALU.mult),
                      r=['f_kd', ('gin', 1)], w=['f_kd'])
                yield
                if d == 0:
                    kb.op('pool', lambda e: e.tensor_copy(out=ksum[:, t0:t0 + tn], in_=f['kd']), r=['f_kd'], w=['ksum'])
                else:
                    kb.op('pool', lambda e: e.tensor_tensor(out=ksum[:, t0:t0 + tn], in0=ksum[:, t0:t0 + tn], in1=f['kd'], op=ALU.add),
                          r=['f_kd', 'ksum'], w=['ksum'])
                kb.op('dve', lambda e: e.tensor_tensor_scan(out=f['P'], data0=mscan[:, 0:tn], data1=f['lw'], initial=0.0,
                                                            op0=ALU.mult, op1=ALU.add), r=['mscan', 'f_lw'], w=['f_P'])
                yield
                tot = v3(f['P'], nck)[:, :, 63:64]
                kb.op('dve', lambda e: e.tensor_tensor(out=v3(f['Q'], nck), in0=tot.to_broadcast([128, nck, 64]), in1=v3(f['P'], nck),
                                                       op=ALU.subtract), r=['f_P'], w=['f_Q'])
                yield
                kb.op('act', lambda e: e.activation(out=gC[:, 0:nck], in_=tot.rearrange("p c o -> p (c o)"), func=AF.Exp),
                      r=['f_P'], w=[f'gC{gs}'])
                yield
                if d == 0:
                    kb.op('dve', lambda e: e.tensor_tensor(out=f['lw'], in0=f['P'], in1=f['lw'], op=ALU.subtract), r=['f_P', 'f_lw'], w=['f_lw'])
                    Lex, Lc, rem = f['lw'], f['P'], f['Q']
                    kx, kc_, kr_ = 'f_lw', 'f_P', 'f_Q'
                else:
                    kb.op('dve', lambda e: e.tensor_tensor(out=f['P'], in0=f['P'], in1=f['lw'], op=ALU.subtract), r=['f_P', 'f_lw'], w=['f_P'])
                    kb.op('dve', lambda e: e.tensor_tensor(out=f['lw'], in0=f['Q'], in1=f['lw'], op=ALU.add), r=['f_Q', 'f_lw'], w=['f_lw'])
                    Lex, Lc, rem = f['Q'], f['lw'], f['P']
                    kx, kc_, kr_ = 'f_Q', 'f_lw', 'f_P'

                def bd_write(name, src, srckey, neg=False):
                    for hh in range(2):
                        o = BD[name][half[hh], 0:nck, hh * 64:(hh + 1) * 64]
                        kb.op('dve' if hh == 0 else 'pool', lambda e, o=o, hh=hh: e.tensor_tensor(
                            out=o, in0=v3(src[half[hh], :], nck), in1=v3(f['E'][half[hh], :], nck), op=ALU.mult),
                            r=[srckey, 'f_E'], w=[f'bd_{name}{gs}'])
                kb.op('dve', lambda e: e.tensor_scalar(out=f['kk'], in0=f['kk'], scalar1=-1.0, scalar2=None, op0=ALU.mult),
                      r=['f_kk'], w=['f_kk'])
                yield
                kb.op('act', lambda e: e.activation(out=f['E'], in_=Lex, func=AF.Exp), r=[kx], w=['f_E'])
                yield
                bd_write('A', f['kk'], 'f_kk', neg=True)
                yield
                kb.op('act', lambda e: e.activation(out=f['E'], in_=Lc, func=AF.Exp), r=[kc_], w=['f_E'])
                yield
                kb.op('dve', lambda e: e.tensor_tensor(out=Rt[:, 0:tn], in0=rg, in1=f['E'], op=ALU.mult), r=[('gin', 0), 'f_E'], w=[f'Rt{gs}'])
                yield
                kb.op('act', lambda e: e.activation(out=f['E'], in_=Lc, func=AF.Exp, scale=-1.0), r=[kc_], w=['f_E'])
                yield
                bd_write('B', f['b'], 'f_b')
                yield
                bd_write('K', f['kd'], 'f_kd')
                yield
                kb.op('act', lambda e: e.activation(out=f['E'], in_=rem, func=AF.Exp), r=[kr_], w=['f_E'])
                yield
                bd_write('BE', f['b'], 'f_b')
                yield
                bd_write('KE', f['kd'], 'f_kd')
                yield
                for hh in range(2):
                    kb.op('pool', lambda e, hh=hh: e.tensor_copy(out=BD['V'][half[hh], 0:nck, hh * 64:(hh + 1) * 64],
                                                                in_=v3(gin[half[hh], 2, 0:tn], nck)), r=[('gin', 2)], w=[f'bd_V{gs}'])

            CH = {'n': 0}

            TTS = {}

            def chunk_off(d, ci, cs, gs):
                BD, Rt = BDs[gs], Rts[gs]
                ms, tk = Ms[cs], Tk[cs]
                kms, ktk = f"Ms{cs}", f"Tk{cs}"
                Axc, Bxc, Kxc = BD['A'][:, ci, :], BD['B'][:, ci, :], BD['K'][:, ci, :]
                Rtc = Rt[:, ci * 64:(ci + 1) * 64]
                pi = nps()
                P_ = ps[pi]
                kb.op('pe', lambda e: e.matmul(P_[:, 0:128], lhsT=Bxc, rhs=Axc, start=True, stop=True), r=[f'bd_B{gs}', f'bd_A{gs}'], w=[('ps', pi)], inc=False)
                kb.op('pe', lambda e: e.matmul(P_[:, 128:192], lhsT=Bxc, rhs=Rtc, start=True, stop=True), r=[f'bd_B{gs}', f'Rt{gs}'], w=[('ps', pi)], inc=False)
                kb.op('pe', lambda e: e.matmul(P_[:, 192:320], lhsT=Kxc, rhs=Axc, start=True, stop=True), r=[f'bd_K{gs}', f'bd_A{gs}'], w=[('ps', pi)], inc=False)
                kb.op('pe', lambda e: e.matmul(P_[:, 320:384], lhsT=Kxc, rhs=Rtc, start=True, stop=True), r=[f'bd_K{gs}', f'Rt{gs}'], w=[('ps', pi)], inc=False)
                kb.op('pe', lambda e: e.matmul(P_[:, 384:512], lhsT=Axc, rhs=Bxc, start=True, stop=True), r=[f'bd_B{gs}', f'bd_A{gs}'], w=[('ps', pi)])
                kb.op('dve', lambda e: e.tensor_tensor(out=ms[:], in0=P_[:, :], in1=mall[:, :], op=ALU.mult),
                      r=[('ps', pi), 'mall'], w=[kms])
                yield
                pi = nps()
                pb = ps[pi][:].bitcast(BF16)
                for i, n in enumerate(('BE', 'KE', 'V')):
                    kb.op('pe', lambda e, i=i, n=n: e.transpose(pb[:, i * 128:(i + 1) * 128], BD[n][:, ci, :], identb[:]),
                          r=[f'bd_{n}{gs}', 'identb'], w=[('ps', pi)], inc=(i == 2))
                kb.op('act', lambda e: e.copy(out=tk[:], in_=pb[:, 0:384]), r=[('ps', pi)], w=[ktk])
                for hh in range(2):
                    kb.op('act', lambda e, hh=hh: e.copy(out=Vx[cs][half[hh], :], in_=tk[half[hh], 256 + hh * 64:320 + hh * 64]),
                          r=[ktk], w=[f"Vx{cs}"])
                dd = DD[cs]
                di = 0
                kdd = lambda i: f"DD{cs}{i}"
                kb.op('pool', lambda e: e.tensor_tensor(out=dd[0][:, 128:256], in0=ms[:, 0:128], in1=lmk[:, 0, :], op=ALU.mult),
                      r=[kms, 'lmk'], w=[kdd(0)])
                kb.op('pool', lambda e: e.tensor_tensor(out=dd[0][:, 128:256], in0=dd[0][:, 128:256], in1=identb[:], op=ALU.add),
                      r=[kdd(0), 'identb'], w=[kdd(0)])
                kb.op('pool', lambda e: e.tensor_tensor(out=dd[0][:, 0:128], in0=ms[:, 384:512], in1=lmk[:, 6, :], op=ALU.mult),
                      r=[kms, 'lmk'], w=[kdd(0)])
                kb.op('pool', lambda e: e.tensor_tensor(out=dd[0][:, 0:128], in0=dd[0][:, 0:128], in1=identb[:], op=ALU.add),
                      r=[kdd(0), 'identb'], w=[kdd(0)])
                yield
                for lv in range(1, 6):
                    pi = nps()
                    kb.op('pe', lambda e, lv=lv, pi=pi, di=di: e.matmul(ps[pi][:, 0:128], lhsT=ms[:, 0:128], rhs=dd[di][:, 0:128],
                                                                    start=True, stop=True), r=[kms, kdd(di)], w=[('ps', pi)])
                    ee = EE[cs]
                    kee = f"EE{cs}"
                    kb.op('dve', lambda e, pi=pi, lv=lv: e.tensor_tensor(out=ee[:], in0=ps[pi][:, 0:128], in1=lmk[:, lv, :], op=ALU.mult),
                          r=[('ps', pi), 'lmk'], w=[kee])
                    yield
                    pi2 = nps()
                    if lv < 5:
                        kb.op('pe', lambda e, pi2=pi2, di=di: e.matmul(ps[pi2][:, 0:128], lhsT=dd[di][:, 128:256], rhs=ee[:],
                                                                      start=True, stop=True), r=[kee, kdd(di)], w=[('ps', pi2)], inc=False)
                    kb.op('pe', lambda e, pi2=pi2, di=di: e.matmul(ps[pi2][:, 128:256], lhsT=ee[:], rhs=dd[di][:, 128:256],
                                                                  start=True, stop=True), r=[kee, kdd(di)], w=[('ps', pi2)])
                    lo = 0 if lv < 5 else 128
                    kb.op('dve', lambda e, pi2=pi2, di=di, lo=lo: e.tensor_tensor(out=dd[1 - di][:, lo:256], in0=ps[pi2][:, lo:256],
                                                                                 in1=dd[di][:, lo:256], op=ALU.add),
                          r=[('ps', pi2), kdd(di)], w=[kdd(1 - di)])
                    di = 1 - di
                    yield
                TTS[cs] = (dd[di][:, 128:256], kdd(di))

            def chunk_on(d, ci, cs, gs, tcol):
                BD, Rt, gC = BDs[gs], Rts[gs], gCs[gs]
                ms, tk = Ms[cs], Tk[cs]
                kms, ktk = f"Ms{cs}", f"Tk{cs}"
                Axc = BD['A'][:, ci, :]
                Rtc = Rt[:, ci * 64:(ci + 1) * 64]
                TT, kTT = TTS[cs]
                ob = CH['n'] % 2
                CH['n'] += 1
                hc = CH['h']
                hn = 1 - hc
                pi = nps()
                kb.op('pe', lambda e: e.matmul(ps[pi][:, 0:64], lhsT=ms[:, 192:320], rhs=Vx[cs][:], start=True, stop=False),
                      r=[kms, f"Vx{cs}"], w=[('ps', pi)])
                kb.op('pe', lambda e: e.matmul(ps[pi][:, 0:64], lhsT=Axc, rhs=Hp[hc][:], start=False, stop=True),
                      r=[f'bd_A{gs}', f"Hp{hc}"], w=[('ps', pi)])
                kb.op('act', lambda e: e.copy(out=Wsb[ob][:], in_=ps[pi][:, 0:64]), r=[('ps', pi)], w=[f"Wsb{ob}"])
                yield
                pi = nps()
                kb.op('pe', lambda e: e.matmul(ps[pi][:, 0:64], lhsT=TT, rhs=Wsb[ob][:], start=True, stop=True),
                      r=[kTT, f"Wsb{ob}"], w=[('ps', pi)])
                kb.op('dve', lambda e: e.tensor_copy(out=Up[ob][:], in_=ps[pi][:, 0:64]), r=[('ps', pi)], w=[f"Up{ob}"])
                for hh in range(2):
                    kb.op('act' if hh == 0 else 'pool', lambda e, hh=hh: (e.copy if hh == 0 else e.tensor_copy)(
                        out=Ubd[ob][half[hh], hh * 64:(hh + 1) * 64], in_=(ps[pi][half[hh], 0:64] if hh == 0 else Up[ob][half[hh], :])),
                        r=[('ps', pi), f"Up{ob}"], w=[f"Ubd{ob}"])
                yield
                piy = nps()
                kb.op('pe', lambda e: e.matmul(ps[piy][:, 0:64], lhsT=Hbd[hc][:], rhs=Rtc, start=True, stop=False),
                      r=[f"Hbd{hc}", f'Rt{gs}'], w=[('ps', piy)])
                kb.op('pe', lambda e: e.matmul(ps[piy][:, 0:64], lhsT=Ubd[ob][:], rhs=ms[:, 128:192], start=False, stop=False),
                      r=[f"Ubd{ob}", kms], w=[('ps', piy)])
                kb.op('pe', lambda e: e.matmul(ps[piy][:, 0:64], lhsT=tk[:, 256:384], rhs=ms[:, 320:384], start=False, stop=True),
                      r=[ktk, kms], w=[('ps', piy)])
                ysl = yT[:, tcol:tcol + 64]
                if d == 0:
                    kb.op('act', lambda e: e.copy(out=ysl, in_=ps[piy][:, 0:64]), r=[('ps', piy)], w=['yT'])
                else:
                    kb.op('dve', lambda e: e.tensor_tensor(out=ysl, in0=ps[piy][:, 0:64], in1=ysl, op=ALU.add), r=[('ps', piy), 'yT'], w=['yT'])
                yield
                pih = nps()
                kb.op('pe', lambda e: e.matmul(ps[pih][:, 0:64], lhsT=tk[:, 0:128], rhs=Up[ob][:], start=True, stop=False),
                      r=[ktk, f"Up{ob}"], w=[('ps', pih)])
                kb.op('pe', lambda e: e.matmul(ps[pih][:, 0:64], lhsT=tk[:, 128:256], rhs=Vx[cs][:], start=False, stop=True),
                      r=[ktk, f"Vx{cs}"], w=[('ps', pih)])
                kb.op('dve', lambda e: e.scalar_tensor_tensor(out=Hm[:], in0=Hm[:], scalar=gC[:, ci:ci + 1], in1=ps[pih][:, 0:64],
                                                              op0=ALU.mult, op1=ALU.add), r=['Hm', f'gC{gs}', ('ps', pih)], w=['Hm'])
                kb.op('act', lambda e: e.copy(out=Hp[hn][:], in_=Hm[:]), r=['Hm'], w=[f"Hp{hn}"])
                for hh in range(2):
                    kb.op('act', lambda e, hh=hh: e.copy(out=Hbd[hn][half[hh], hh * 64:(hh + 1) * 64], in_=Hm[half[hh], :]),
                          r=['Hm'], w=[f"Hbd{hn}"])
                CH['h'] = hn
                yield

            def out_stage(pj, t0, tn):
                Rt = Rts[0]
                for i, cidx in enumerate((pj, 8 + pj, 14)):
                    kb.dma('sp', gin[:, i, 0:tn], pA_d[cidx][:, t0:t0 + tn], r=[('pA', cidx)], w=[('gin', i)])
                rg, vg, glg = (gin[:, i, 0:tn] for i in range(3))
                f = {n: F[n][:, 0:tn] for n in F}
                ysl = yT[:, t0:t0 + tn]
                pi = nps()
                kb.op('pe', lambda e: e.matmul(ps[pi][:, 0:tn], lhsT=blkf[:, :], rhs=ysl, start=True, stop=True), r=['blkf', 'yT'], w=[('ps', pi)])
                kb.op('dve', lambda e: e.scalar_tensor_tensor(out=f['P'], in0=ps[pi][:, 0:tn], scalar=-1.0 / 64, in1=ysl,
                                                              op0=ALU.mult, op1=ALU.add), r=[('ps', pi), 'yT'], w=['f_P'])
                kb.op('pool', lambda e: e.tensor_tensor(out=f['Q'], in0=f['P'], in1=f['P'], op=ALU.mult), r=['f_P'], w=['f_Q'])
                pi = nps()
                kb.op('pe', lambda e: e.matmul(ps[pi][:, 0:tn], lhsT=blkf[:, :], rhs=f['Q'], start=True, stop=True), r=['blkf', 'f_Q'], w=[('ps', pi)])
                kb.op('act', lambda e: e.activation(out=f['E'], in_=ps[pi][:, 0:tn], func=AF.Sqrt, bias=gne[:, 0:1], scale=1.0 / 64),
                      r=[('ps', pi), 'gne'], w=['f_E'])
                kb.op('dve', lambda e: e.reciprocal(out=f['E'], in_=f['E']), r=['f_E'], w=['f_E'])
                kb.op('dve', lambda e: e.tensor_tensor(out=f['P'], in0=f['P'], in1=f['E'], op=ALU.mult), r=['f_P', 'f_E'], w=['f_P'])
                kb.op('dve', lambda e: e.tensor_scalar(out=f['P'], in0=f['P'], scalar1=rwp[:, 58 + pj:59 + pj], scalar2=rwp[:, 62 + pj:63 + pj],
                                                       op0=ALU.mult, op1=ALU.add), r=['f_P', 'rwp'], w=['f_P'])
                kb.op('pool', lambda e: e.tensor_tensor(out=Rt[:, 0:tn], in0=rg, in1=ksum[:, t0:t0 + tn], op=ALU.mult),
                      r=[('gin', 0), 'ksum'], w=['Rt0'])
                pi = nps()
                kb.op('pe', lambda e: e.matmul(ps[pi][:, 0:tn], lhsT=blkRK[:, :], rhs=Rt[:, 0:tn], start=True, stop=True),
                      r=['blkRK', 'Rt0'], w=[('ps', pi)])
                kb.op('dve', lambda e: e.tensor_tensor(out=f['Q'], in0=ps[pi][:, 0:tn], in1=vg, op=ALU.mult), r=[('ps', pi), ('gin', 1)], w=['f_Q'])
                kb.op('dve', lambda e: e.tensor_tensor(out=f['P'], in0=f['P'], in1=f['Q'], op=ALU.add), r=['f_P', 'f_Q'], w=['f_P'])
                pi = nps()
                kb.op('pe', lambda e: e.matmul(ps[pi][:, 0:tn], lhsT=g2s[:, :], rhs=glg, start=True, stop=True),
                      r=['g2s', ('gin', 2)], w=[('ps', pi)])
                kb.op('dve', lambda e: e.tensor_tensor(out=mixT[:, pj, t0:t0 + tn], in0=ps[pi][:, 0:tn], in1=f['P'], op=ALU.mult),
                      r=[('ps', pi), 'f_P'], w=[('mixT', pj)])

            for pj in range(4):
                pcs = slice(pj * 128, (pj + 1) * 128)
                kb.dma('pool', w2s[:], w2_d[j][:, pcs], w=['w2s'])
                kb.dma('pool', a2s[:], a2_d[j][:, pcs], w=['a2s'])
                kb.dma('pool', g2s[:], g2_d[j][:, pcs], w=['g2s'])
                kb.op('dve', lambda e: e.tensor_scalar(out=blkRK[:], in0=blkf[:], scalar1=rwp[:, 54 + pj:55 + pj], scalar2=None,
                                                       op0=ALU.mult), r=['blkf', 'rwp'], w=['blkRK'])
                for d in range(2):
                    kb.dma('pool', mall[:], mall_d[:, d, :], w=['mall'])
                    kb.dma('pool', lmk[:], lmk_d[:, d, :, :], w=['lmk'])
                    CH['h'] = 0
                    kb.op('dve', lambda e: e.memset(Hm[:], 0.0), w=['Hm'])
                    kb.op('dve', lambda e: e.memset(Hp[0][:], 0.0), w=['Hp0'])
                    for i in range(2):
                        kb.op('pool', lambda e, i=i: e.memset(Hbd[i][:], 0.0), w=[f"Hbd{i}"])
                    order = G256 if d == 0 else [G256[0]] + G256[:0:-1]

                    def drain(gens):
                        while gens:
                            for g in list(gens):
                                try:
                                    next(g)
                                except StopIteration:
                                    gens.remove(g)
                    def offs(gi):
                        cis = list(range(4) if d == 0 else range(3, -1, -1))
                        return [chunk_off(d, ci, (gi % 2) * 4 + k, gi % 2) for k, ci in enumerate(cis)]

                    def on_then_gen(gi):
                        t0 = order[gi][0]
                        cis = list(range(4) if d == 0 else range(3, -1, -1))
                        for k, ci in enumerate(cis):
                            yield from chunk_on(d, ci, (gi % 2) * 4 + k, gi % 2, t0 + ci * 64)
                        if gi + 2 < len(order):
                            yield from gen(pj, d, order[gi + 2][0], order[gi + 2][1], gi % 2)
                    drain([gen(pj, d, order[0][0], order[0][1], 0)])
                    drain(offs(0) + [gen(pj, d, order[1][0], order[1][1], 1)])
                    for gi in range(len(order)):
                        gens = [on_then_gen(gi)]
                        if gi + 1 < len(order):
                            gens += offs(gi + 1)
                        drain(gens)
                for (t0, tn) in G256:
                    out_stage(pj, t0, tn)
            kb.barrier(keys_all + [('gin', i) for i in range(5)])

        def even_mixer(L, mixT):
            with uT_scope():
                prep(0)
                with contextlib.ExitStack() as es2:
                    even_B(L, mixT, es2)
                if rwkv:
                    with contextlib.ExitStack() as es2:
                        rwkv_project(L, es2)
            if rwkv:
                with contextlib.ExitStack() as es2:
                    rwkv_passes(L, mixT, es2)
            else:
                for c in range(4):
                    kb.op('pool', lambda e, c=c: e.memset(mixT[:, c, :], 0.0), w=[('mixT', c)])

        def even_B(L, mixT, es2):
            j = L // 2
            sb2 = lambda name, shape, dt=F32: es2.enter_context(SBT(name, list(shape), dt))
            wsrc = wev_d[j]
            wb = {'n': 0, 'bufs': [sb2(f"wb{i}", (128, 8, 128), BF16) for i in range(2)]}
            A = {'g': 0, 's': 0, 'p': 0, 'PT': [sb2(f"PT{i}", (128, 512), BF16) for i in range(3)],
                 'rden': sb2("rden", (128, 512))}
            with contextlib.ExitStack() as es3:
                sb3 = lambda name, shape, dt=F32: es3.enter_context(SBT(name, list(shape), dt))
                kA = sb3("kA", (128, TOK), BF16)
                vt = sb3("vt", (128, NT, 192), BF16)
                kb.op('pool', lambda e: e.memset(vt[:, :, 64:128], 1.0), w=['vt'])
                wmask = sb3("wmask", (128, 6, 512), BF16)
                RP['C'] = sb3("ropeC", (128, TOK), BF16)
                RP['S'] = sb3("ropeS", (128, TOK), BF16)
                kb.dma('pool', RP['C'][:], ropeC_d, w=['rope'])
                kb.dma('pool', RP['S'][:], ropeS_d, w=['rope'])
                esk = sb3("esk", (1, 8))
                eskb = sb3("eskb", (128, 8))
                qa = sb3("qa", (128, TOK), BF16)
                qp = sb3("qp", (128, TOK), BF16)
                kb.dma('pool', wmask[:], wmask_d, w=['wmask'])
                kb.dma('sp', esk[:], sink_d[j:j + 1, :], w=['esk'])
                kb.op('act', lambda e: e.activation(out=esk[:], in_=esk[:], func=AF.Exp), r=['esk'], w=['esk'])
                pi0 = nps()
                kb.op('pe', lambda e: e.matmul(ps[pi0][:, 0:8], lhsT=ones_f[0:1, :], rhs=esk[0:1, :], start=True, stop=True),
                      r=['ones_f', 'esk'], w=[('ps', pi0)])
                kb.op('dve', lambda e: e.tensor_copy(out=eskb[:], in_=ps[pi0][:, 0:8]), r=[('ps', pi0)], w=['eskb'])

                def evx(dst, key):
                    def f(c, t0, tn, pi):
                        kb.op('dve', lambda e: e.tensor_copy(out=dst[:, t0:t0 + tn], in_=ps[pi][:, 0:tn]),
                              r=[('ps', pi)], w=[key])
                    return f
                proj_fm(wsrc[:, 23 * 128:24 * 128], 1, wb, evx(kA, ('kx', 0)))
                proj_fm(wsrc[:, 24 * 128:25 * 128], 1, wb, evx(qp, ('qx', 1)))
                rope(kA, qp, ('kx', 0), ('qx', 1))

                def evv(t, pi):
                    kb.op('act', lambda e: e.copy(out=vt[:, t, 0:64], in_=ps[pi][:, 0:64]), r=[('ps', pi)], w=['vt'])
                    kb.op('dve', lambda e: e.tensor_copy(out=vt[:, t, 128:192], in_=ps[pi][:, 64:128]), r=[('ps', pi)], w=['vt'])
                proj_tm(wsrc[:, 25 * 128:26 * 128], 128, wb, evv)

                for qc in range(4):
                    def evq(c, t0, tn, pi):
                        dst = (qa, qp)[c]
                        kb.op('dve' if c == 0 else 'act', lambda e: (e.tensor_copy if c == 0 else e.copy)(
                            out=dst[:, t0:t0 + tn], in_=ps[pi][:, 0:tn]), r=[('ps', pi)], w=[('qx', c)])
                    proj_fm(wsrc[:, (15 + qc) * 128:(16 + qc) * 128], 1, wb, lambda c, t0, tn, pi: evq(0, t0, tn, pi))
                    proj_fm(wsrc[:, (19 + qc) * 128:(20 + qc) * 128], 1, wb, lambda c, t0, tn, pi: evq(1, t0, tn, pi))
                    rope(qa, qp, ('qx', 0), ('qx', 1))
                    for hh in range(2):
                        hd = qc + 4 * hh
                        base = hh * 64
                        mrow = slice(base, base + 64)
                        attn_group(A, qa, ('qx', 0), base, kA, ('kx', 0), vt, 'vt', 0, 256, [(0, None), (1, None)],
                                   mixT[mrow, 4 + qc, 0:256], ('mixT', 4 + qc), seed=eskb[:, hd:hd + 1])
                        for Gq in range(4):
                            keys = [(0, None), (1, None)]
                            for dl in range(-1, 5):
                                lt = 4 * Gq + dl
                                if 0 <= lt < 16:
                                    c0, c1 = max(0, dl - 1) * 128, min(4, dl + 2) * 128
                                    keys.append((2 + lt, (wmask[:, dl + 1, c0:c1], 'wmask'), (c0, c1)))
                            q0 = 256 + 512 * Gq
                            attn_group(A, qa, ('qx', 0), base, kA, ('kx', 0), vt, 'vt', q0, 512, keys,
                                       mixT[mrow, 4 + qc, q0:q0 + 512], ('mixT', 4 + qc), seed=eskb[:, hd:hd + 1])
                kb.barrier([('kx', 0), 'vt', 'wmask', 'esk', 'eskb', ('qx', 0), ('qx', 1), 'rope'])
            kb.barrier([('wb', 0), ('wb', 1), ('PT', 0), ('PT', 1), ('PT', 2), 'rden'])


        def odd_mixer(L, mixT):
            j = L // 2
            wsrc = wod_d[j]
            with uT_scope():
                prep(0)
                with contextlib.ExitStack() as es2:
                    sb2 = lambda name, shape, dt=F32: es2.enter_context(SBT(name, list(shape), dt))
                    wb = {'n': 0, 'bufs': [sb2(f"wb{i}", (128, 8, 128), BF16) for i in range(2)]}
                    A = {'g': 0, 's': 0, 'p': 0, 'PT': [sb2(f"PT{i}", (128, 512), BF16) for i in range(3)],
                         'rden': sb2("rden", (128, 512))}
                    qa = sb2("qa", (128, TOK), BF16)

                    def evx(dst, key, eng='dve'):
                        def f(c, t0, tn, pi):
                            kb.op(eng, lambda e: (e.tensor_copy if eng == 'dve' else e.copy)(out=dst[:, t0:t0 + tn], in_=ps[pi][:, 0:tn]),
                                  r=[('ps', pi)], w=[key])
                        return f
                    with contextlib.ExitStack() as es3:
                        sb3 = lambda name, shape, dt=F32: es3.enter_context(SBT(name, list(shape), dt))
                        vtc = sb3("vtc", (128, NT, 192), BF16)
                        kb.op('pool', lambda e: e.memset(vtc[:, :, 64:128], 1.0), w=['vtc'])
                        kc_ = sb3("kc", (128, TOK), BF16)
                        bst = sb3("bst", (128, (NCT + 1) * 128))
                        Et = sb3("Et", (128, NCT + 1, 128), BF16)
                        for jp in range(4):
                            proj_fm(wsrc[:, jp * 128:(jp + 1) * 128], 1, wb, evx(qa, ('qx', 0)))
                            proj_fm(wsrc[:, (4 + jp) * 128:(5 + jp) * 128], 1, wb, evx(kc_, ('kx', 0), 'act'))

                            def evv(t, pi):
                                kb.op('dve', lambda e: e.tensor_copy(out=vtc[:, t, 0:64], in_=ps[pi][:, 0:64]), r=[('ps', pi)], w=['vtc'])
                                kb.op('act', lambda e: e.copy(out=vtc[:, t, 128:192], in_=ps[pi][:, 64:128]), r=[('ps', pi)], w=['vtc'])
                            proj_tm(wsrc[:, (8 + jp) * 128:(9 + jp) * 128], 128, wb, evv)
                            for hh in range(2):
                                hd = jp * 2 + hh
                                base = hh * 64
                                mrow = slice(base, base + 64)
                                kb.dma('sp', bst[:], cbias_d[j, hd], w=['bst'])
                                kb.op('act', lambda e: e.activation(out=Et[:].rearrange("p a b -> p (a b)"), in_=bst[:], func=AF.Exp),
                                      r=['bst'], w=['Et'])
                                if L < DEPTH - 1:
                                    attn_group(A, qa, ('qx', 0), base, kc_, ('kx', 0), vtc, 'vtc', 0, 256, [(0, None), (1, None)],
                                               mixT[mrow, jp, 0:256], ('mixT', jp))
                                for m in range(0, 16, 2):
                                    keys = [(0, None), (1, None)]
                                    ta = {m + dl: ti for (dl, ti) in C_TILES[c_class(m)]}
                                    tb = {m + 1 + dl: ti for (dl, ti) in C_TILES[c_class(m + 1)]}
                                    for kt in sorted(set(ta) | set(tb)):
                                        ia, ib = ta.get(kt), tb.get(kt)
                                        pieces = []
                                        if ia is not None:
                                            pieces.append((0, 128, Et[:, ia, :]))
                                        if ib is not None:
                                            pieces.append((128, 256, Et[:, ib, :]))
                                        keys.append((2 + kt, (pieces, 'Et'), (0 if ia is not None else 128, 256 if ib is not None else 128)))
                                    q0 = 256 + 128 * m
                                    attn_group(A, qa, ('qx', 0), base, kc_, ('kx', 0), vtc, 'vtc', q0, 256, keys,
                                               mixT[mrow, jp, q0:q0 + 256], ('mixT', jp))
                        kb.barrier(['vtc', 'bst', 'Et', ('kx', 0)])
                    with contextlib.ExitStack() as es3:
                        sb3 = lambda name, shape, dt=F32: es3.enter_context(SBT(name, list(shape), dt))
                        kA = sb3("kA", (128, TOK), BF16)
                        vt = sb3("vt", (128, NT, 192), BF16)
                        kb.op('pool', lambda e: e.memset(vt[:, :, 64:128], 1.0), w=['vt'])
                        qp = sb3("qp", (128, TOK), BF16)
                        RP['C'] = sb3("ropeC", (128, TOK), BF16)
                        RP['S'] = sb3("ropeS", (128, TOK), BF16)
                        blkf = sb3("blkf", (128, 128))
                        dg = sb3("dg", (128, 4))
                        sq = sb3("sq", (128, 512))
                        rs = sb3("rs", (128, 512))
                        kb.dma('pool', RP['C'][:], ropeC_d, w=['rope'])
                        kb.dma('pool', RP['S'][:], ropeS_d, w=['rope'])
                        kb.dma('sp', blkf[:], blkf_d, w=['blkf'])
                        kb.dma('sp', dg[:], dgain_d[j], w=['dg'])

                        def evn(dst, key, gcol):
                            def f(c, t0, tn, pi):
                                kb.op('act', lambda e: e.activation(out=sq[:, 0:tn], in_=ps[pi][:, 0:tn], func=AF.Square),
                                      r=[('ps', pi)], w=['sq'])
                                pi2 = nps()
                                kb.op('pe', lambda e: e.matmul(ps[pi2][:, 0:tn], lhsT=blkf[:, :], rhs=sq[:, 0:tn], start=True, stop=True),
                                      r=['blkf', 'sq'], w=[('ps', pi2)])
                                kb.op('act', lambda e: e.activation(out=rs[:, 0:tn], in_=ps[pi2][:, 0:tn], func=AF.Sqrt,
                                                                    bias=epsT[:, 0:1], scale=1.0 / 64), r=[('ps', pi2), 'epsT'], w=['rs'])
                                kb.op('dve', lambda e: e.reciprocal(out=rs[:, 0:tn], in_=rs[:, 0:tn]), r=['rs'], w=['rs'])
                                kb.op('dve', lambda e: e.scalar_tensor_tensor(out=dst[:, t0:t0 + tn], in0=ps[pi][:, 0:tn],
                                                                              scalar=dg[:, gcol:gcol + 1], in1=rs[:, 0:tn],
                                                                              op0=ALU.mult, op1=ALU.mult),
                                      r=[('ps', pi), 'dg', 'rs'], w=[key])
                            return f
                        proj_fm(wsrc[:, 20 * 128:21 * 128], 1, wb, evn(kA, ('kx', 1), 2))
                        proj_fm(wsrc[:, 21 * 128:22 * 128], 1, wb, evn(qp, ('qx', 1), 3))
                        rope(kA, qp, ('kx', 1), ('qx', 1))

                        def evv2(t, pi):
                            kb.op('act', lambda e: e.copy(out=vt[:, t, 0:64], in_=ps[pi][:, 0:64]), r=[('ps', pi)], w=['vt'])
                            kb.op('dve', lambda e: e.tensor_copy(out=vt[:, t, 128:192], in_=ps[pi][:, 64:128]), r=[('ps', pi)], w=['vt'])
                        proj_tm(wsrc[:, 22 * 128:23 * 128], 128, wb, evv2)
                        for qc in range(4):
                            proj_fm(wsrc[:, (12 + qc) * 128:(13 + qc) * 128], 1, wb, evn(qa, ('qx', 0), 0))
                            proj_fm(wsrc[:, (16 + qc) * 128:(17 + qc) * 128], 1, wb, evn(qp, ('qx', 1), 1))
                            rope(qa, qp, ('qx', 0), ('qx', 1))
                            for hh in range(2):
                                base = hh * 64
                                mrow = slice(base, base + 64)
                                if L < DEPTH - 1:
                                    attn_group(A, qa, ('qx', 0), base, kA, ('kx', 1), vt, 'vt', 0, 256, [(0, None), (1, None)],
                                               mixT[mrow, 4 + qc, 0:256], ('mixT', 4 + qc))
                                for Gq in range(4):
                                    q0 = 256 + 512 * Gq
                                    attn_group(A, qa, ('qx', 0), base, kA, ('kx', 1), vt, 'vt', q0, 512,
                                               [(t, None) for t in range(NT)], mixT[mrow, 4 + qc, q0:q0 + 512], ('mixT', 4 + qc))
                        kb.barrier([('kx', 1), 'vt', ('qx', 1), 'rope', 'blkf', 'dg', 'sq', 'rs'])
                    kb.barrier([('wb', 0), ('wb', 1), ('PT', 0), ('PT', 1), ('PT', 2), 'rden', ('qx', 0), ('kx', 0)])

        def mixer(L):
            with contextlib.ExitStack() as es2:
                mixT = es2.enter_context(SBT("mixT", [128, 8, TOK], BF16))
                if L % 2 == 0:
                    even_mixer(L, mixT)
                else:
                    odd_mixer(L, mixT)
                if dbg == 'mixT' and L == 0 or (dbg == 'mixT1' and L == 1) or (dbg == 'mixT2' and L == 2) or (dbg == 'mixT3' and L == 3):
                    for c in range(8):
                        kb.dma('sp', dbg_d[:, c, :], mixT[:, c, :], r=[('mixT', c)], w=[('dbg', c)])
                outproj(L, mixT)
                kb.barrier([('mixT', c) for c in range(8)])

        for L in range(nlayers):
            adaln(L)
            ffn_convert(L)
            if mix:
                mixer(L)
            with uT_scope():
                prep(2)
                ffn(L)

        ov = out_d.rearrange("(t p) d -> p t d", p=128)
        for t in range(16):
            kb.dma('sp', ov[:, t, :], h[:, 2 + t, :], r=[('h', 2 + t)], w=[('out', t)])
        kb.wait_all('sp', [('out', t) for t in range(16)])
        print("instructions:", kb.nins, kb.cnt)
    return nc


def _consts():
    t = np.arange(2048)
    inv = 10000.0 ** (-np.arange(16, dtype=np.float32) / 16)
    row = (t // 64).astype(np.float32)[:, None] * inv
    col = (t % 64).astype(np.float32)[:, None] * inv
    C = np.ones((64, TOK), np.float32)
    S = np.zeros((64, TOK), np.float32)
    for d in range(64):
        ang = (row if d < 32 else col)[:, d % 16]
        C[d, 256:] = np.cos(ang)
        S[d, 256:] = np.sin(ang) * (-1.0 if (d % 32) < 16 else 1.0)
    ropeC = np.concatenate([C, C], 0)
    ropeS = np.concatenate([S, S], 0)
    i = np.arange(128)[:, None]
    c = np.arange(512)[None, :]
    wmask = np.stack([(np.abs(dl * 128 + i - c) <= 128).astype(np.float32) for dl in range(-1, 5)], axis=1)
    s_ = np.arange(64)[:, None]
    t_ = np.arange(64)[None, :]
    mall = np.zeros((128, 2, 512), np.float32)
    for d in range(2):
        strict = (s_ < t_) if d == 0 else (s_ > t_)
        incl = (s_ <= t_) if d == 0 else (s_ >= t_)
        for hh in range(2):
            rs = slice(hh * 64, hh * 64 + 64)
            for off in (0, 192):
                mall[rs, d, off + hh * 64: off + hh * 64 + 64] = strict
                mall[rs, d, off + 128: off + 192] = incl
            mall[rs, d, 384 + hh * 64: 384 + hh * 64 + 64] = strict.T
    ii = np.arange(64)
    lmk = np.zeros((128, 2, 7, 128), np.float32)
    for lv in range(6):
        b = 1 << lv
        LMv = ((ii[:, None] // (2 * b) == ii[None, :] // (2 * b)) & ((ii[:, None] // b) % 2 == 1) & ((ii[None, :] // b) % 2 == 0))
        for hh in range(2):
            rs = slice(hh * 64, hh * 64 + 64)
            if lv == 0:
                lmk[rs, 0, 0, rs] = LMv.T
                lmk[rs, 1, 0, rs] = LMv
                lmk[rs, 0, 6, rs] = LMv
                lmk[rs, 1, 6, rs] = LMv.T
            else:
                lmk[rs, 0, lv, rs] = LMv
                lmk[rs, 1, lv, rs] = LMv.T
    mscan = np.ones((128, 512), np.float32)
    mscan[:, ::64] = 0
    blkf = np.zeros((128, 128), np.float32)
    blkf[:64, :64] = 1
    blkf[64:, 64:] = 1
    return ropeC, ropeS, wmask, mall, mscan, blkf, lmk


def _rwp(inp):
    out = np.zeros((2, 128, 66), np.float32)
    for j in range(2):
        mu = np.stack([inp["a_mu_prev"][j], inp["a_mu_next"][j]], -1)
        out[j, :, 0:30] = mu.reshape(15, 128, 2).transpose(1, 0, 2).reshape(128, 30)
        pl = lambda v: v.reshape(4, 128).T
        out[j, :, 30:38] = np.stack([pl(inp["a_w0"][j, d]) for d in range(2)], -1).reshape(128, 8)
        out[j, :, 38:46] = np.stack([pl(inp["a_a0"][j, d]) for d in range(2)], -1).reshape(128, 8)
        out[j, :, 46:50] = pl(inp["a_k_k"][j])
        out[j, :, 50:54] = pl(inp["a_k_a"][j])
        out[j, :, 54:58] = pl(inp["a_r_k"][j].reshape(512))
        out[j, :, 58:62] = pl(inp["a_gn_w"][j])
        out[j, :, 62:66] = pl(inp["a_gn_b"][j])
    return out


def _c_allowed(tq, tk):
    r, c = tq // 64, tq % 64
    kr, kc = tk // 64, tk % 64
    r0 = np.clip(r - 4, 0, 24)
    c0 = np.clip(c - 8, 0, 48)
    ok = (kr >= r0) & (kr < r0 + 8) & (kc >= c0) & (kc < c0 + 16)
    return ok, kr - r + 7, np.clip(kc - c + 15, 0, 30)


def c_class(m):
    return {0: 0, 1: 1, 14: 3, 15: 4}.get(m, 2)


def _c_tiles():
    tiles, n = {}, 0
    for cls, m in enumerate((0, 1, 7, 14, 15)):
        lst = []
        tq = m * 128 + np.arange(128)[None, :]
        for dl in range(-3, 4):
            if not (0 <= m + dl < 16):
                continue
            tk = (m + dl) * 128 + np.arange(128)[:, None]
            ok, _, _ = _c_allowed(tq, tk)
            if ok.any():
                lst.append((dl, n))
                n += 1
        tiles[cls] = lst
    return tiles, n


C_TILES, NCT = _c_tiles()


def _cbias(rpb):
    out = np.full((2, 8, 128, NCT + 1, 128), -30000.0, np.float32)
    for cls, m in enumerate((0, 1, 7, 14, 15)):
        tq = m * 128 + np.arange(128)[None, :]
        for (dl, ti) in C_TILES[cls]:
            tk = (m + dl) * 128 + np.arange(128)[:, None]
            ok, dr, dc = _c_allowed(tq, tk)
            dr = np.clip(dr, 0, 14)
            g = rpb[:, :, dr, dc]
            out[:, :, :, ti, :] = np.where(ok[None, None], g, np.float32(-30000.0))
    return out.reshape(2, 8, 128, (NCT + 1) * 128)


def _odd_cols():
    cq = np.arange(512)
    ck = 512 + np.arange(512)
    cv = 1024 + np.arange(512)
    q = 1536 + np.arange(512)
    k = 2048 + np.arange(128)
    v = 2176 + np.arange(128)
    pm = _perm64()
    hq = lambda h: q[h * 64:(h + 1) * 64]
    cols = list(cq) + list(ck) + list(cv)
    for c in range(4):
        cols += list(hq(c)) + list(hq(4 + c))
    for c in range(4):
        cols += list(hq(c)[pm]) + list(hq(4 + c)[pm])
    k0, k1 = k[:64], k[64:]
    cols += list(k0) + list(k1)
    cols += list(k0[pm]) + list(k1[pm])
    cols += list(v)
    return np.array(cols)


def _dgain(inp):
    pm = _perm64()
    out = np.zeros((2, 128, 4), np.float32)
    for j in range(2):
        qg, kg = inp["d_q_gain"][j], inp["d_k_gain"][j]
        for i, v in enumerate((qg, qg[pm], kg, kg[pm])):
            out[j, :, i] = np.concatenate([v, v])
    return out


def _perm64():
    p = np.arange(64)
    return np.where((p % 32) < 16, p + 16, p - 16)


def _even_cols():
    A_IN = 1920
    cols = list(range(A_IN))
    q = A_IN + np.arange(512)
    k = A_IN + 512 + np.arange(128)
    v = A_IN + 640 + np.arange(128)
    pm = _perm64()
    hq = lambda h: q[h * 64:(h + 1) * 64]
    for c in range(4):
        cols += list(hq(c)) + list(hq(4 + c))
    for c in range(4):
        cols += list(hq(c)[pm]) + list(hq(4 + c)[pm])
    k0, k1 = k[:64], k[64:]
    cols += list(k0) + list(k1)
    cols += list(k0[pm]) + list(k1[pm])
    cols += list(v)
    return np.array(cols)


def host_inputs(inp, b, consts=None):
    f = lambda a: np.ascontiguousarray(a, dtype=np.float32)
    ropeC, ropeS, wmask, mall, mscan, blkf, lmk = consts if consts is not None else _consts()
    cc = np.stack([inp["c"][b], inp["c_ctx"]], axis=-1)
    cvec = cc.reshape(8, 128, 2).transpose(1, 0, 2)
    gains = np.stack([inp["g_pre_mix"], inp["g_post_mix"], inp["g_pre_ff"], inp["g_post_ff"]], axis=1)
    sel = np.zeros((2, 2, 128), np.float32)
    sel[0, 0] = 1
    sel[1, 1] = 1
    return {
        "x": f(inp["x"][b]), "ctx": f(inp["ctx"][b]), "cvec": f(cvec),
        "w_ada": f(inp["w_ada"]), "b_ada": f(inp["b_ada"]),
        "gains2": f(np.stack([gains] * 2, axis=1)),
        "w_ff1": f(inp["w_ff1"]), "w_ff2": f(inp["w_ff2"]),
        "identf_in": np.eye(128, dtype=np.float32), "sel_in": sel,
        "w_out": f(inp["w_out"]), "w_even_x": f(inp["w_in_even"][:, :, _even_cols()]),
        "rwp_in": _rwp(inp), "a_w2s": f(inp["a_w2"].reshape(2, 128, 512)), "a_a2s": f(inp["a_a2"].reshape(2, 128, 512)),
        "a_g2": f(inp["a_g2"]), "mall_in": f(mall), "mscan_in": f(mscan), "lmk_in": f(lmk), "blkf_in": f(blkf),
        "w_odd_x": f(inp["w_in_odd"][:, :, _odd_cols()]), "cbias_in": _cbias(inp["c_rpb"]), "dgain_in": _dgain(inp),
        "wmask_in": f(wmask), "b_sink": f(inp["b_sink"]), "ropeC_in": f(ropeC), "ropeS_in": f(ropeS),
    }


def kernel(**inputs):
    inp = {k: np.asarray(v) for k, v in inputs.items()}
    nc = build()
    in_maps = [host_inputs(inp, b) for b in range(8)]
    res = run_bass_kernel_spmd(nc, in_maps, core_ids=list(range(8)))
    return np.stack([r["out"] for r in res.results], axis=0).astype(np.float32)
```
